# Optimizing a Trainium2 kernel written in Bass

```python
import math
import jax
import jax.numpy as jnp
from jax import lax
import numpy as np

D_MODEL = 1024
BATCH = 16
SEQ = 4096
DEPTH = 2

D_MIX = D_MODEL
D_FF = 4 * D_MODEL
HEAD_DIM = 64
C_R = D_MIX // 4
H_R = C_R // HEAD_DIM
R_DECAY = 32
R_AAA = 32
R_GATE = 64
RWKV_GN_EPS = 64e-5
C_M = D_MIX // 2
H_M = C_M // HEAD_DIM
SSM_GROUPS = 2
SSM_STATE = 128
CONV_K = 4
CONV_DIM = C_M + 2 * SSM_GROUPS * SSM_STATE
CHUNK = 128
SSM_NORM_EPS = 1e-5
C_F = D_MIX - C_R - C_M
H_F = C_F // HEAD_DIM
Q_BLOCK = 128
RWKV_COLS = 3 * C_R + R_DECAY + R_AAA + R_GATE
SSM_COLS = C_M + CONV_DIM + H_M
FOX_COLS = 3 * C_F + H_F
D_IN = RWKV_COLS + SSM_COLS + FOX_COLS
NORM_EPS = 1e-6

kernel_name = 'hybrid_rwkv7_mamba2_fox_trunk'


def rms_norm(x, g, eps=NORM_EPS):
    xf = x.astype(jnp.float32)
    y = xf * lax.rsqrt(jnp.mean(xf * xf, axis=-1, keepdims=True) + eps)
    return (y * g.astype(jnp.float32)).astype(x.dtype)


def token_shift(t):
    return jnp.pad(t, ((0, 0), (1, 0), (0, 0)))[:, :-1]


def causal_depthwise_conv(x, w, b):
    out = lax.conv_general_dilated(
        x, w[:, None, :].astype(x.dtype), window_strides=(1,),
        padding=[(CONV_K - 1, 0)], dimension_numbers=('NWC', 'WIO', 'NWC'),
        feature_group_count=x.shape[-1])
    return out + b


def rwkv7_recurrence(r, w, k, v, kk, b):
    def step(S, inp):
        r_t, w_t, k_t, v_t, kk_t, b_t = inp
        sa = jnp.einsum('bhij,bhj->bhi', S, kk_t)
        S = (S * w_t[:, :, None, :] - sa[..., None] * b_t[:, :, None, :]
             + v_t[..., None] * k_t[:, :, None, :])
        return S, jnp.einsum('bhij,bhj->bhi', S, r_t)
    Bsz, _, H, N = r.shape
    s0 = jnp.zeros((Bsz, H, N, N), jnp.float32)
    xs = tuple(jnp.moveaxis(t, 1, 0) for t in (r, w, k, v, kk, b))
    _, y = lax.scan(step, s0, xs)
    return jnp.moveaxis(y, 0, 1)


def rwkv7_group(p, mu, w0, w2, a0, a2, g2, k_k, k_a, r_k, ln_w, ln_b):
    Bsz, T, _ = p.shape
    p = p + (token_shift(p) - p) * mu
    r, k, v, w_lo, a_lo, g_lo = jnp.split(
        p, [C_R, 2 * C_R, 3 * C_R, 3 * C_R + R_DECAY, 3 * C_R + R_DECAY + R_AAA], axis=-1)
    logw = (w0 + jnp.tanh(w_lo) @ w2).astype(jnp.float32)
    decay = jnp.exp(-jnp.exp(-jax.nn.softplus(-logw) - 0.5))
    a = jax.nn.sigmoid(a0 + a_lo @ a2)
    g = jax.nn.sigmoid(g_lo) @ g2
    kk = k * k_k
    k = k * (1.0 + (a - 1.0) * k_a)

    def heads(t):
        return t.reshape(Bsz, T, H_R, HEAD_DIM).astype(jnp.float32)

    kk = heads(kk)
    kk = kk / jnp.maximum(jnp.sqrt(jnp.sum(kk * kk, axis=-1, keepdims=True)), 1e-12)
    rh, kh, vh, ah, dh = heads(r), heads(k), heads(v), heads(a), heads(decay)
    y = rwkv7_recurrence(rh, dh, kh, vh, kk, kk * ah)
    m = jnp.mean(y, axis=-1, keepdims=True)
    var = jnp.mean(jnp.square(y - m), axis=-1, keepdims=True)
    y = (y - m) * lax.rsqrt(var + RWKV_GN_EPS)
    y = y.reshape(Bsz, T, C_R) * ln_w.astype(jnp.float32) + ln_b.astype(jnp.float32)
    bonus = jnp.sum(rh * kh * r_k.astype(jnp.float32), axis=-1, keepdims=True) * vh
    y = (y + bonus.reshape(Bsz, T, C_R)) * g.astype(jnp.float32)
    return y.astype(p.dtype)


def ssd_chunked(xh, dt, A, Bm, Cm):
    Bsz, T, H, P = xh.shape
    G, N = Bm.shape[2], Bm.shape[3]
    E = H // G
    nc = T // CHUNK
    x = (xh * dt[..., None]).reshape(Bsz, nc, CHUNK, G, E, P)
    a = (dt * A).reshape(Bsz, nc, CHUNK, G, E)
    Bc = Bm.reshape(Bsz, nc, CHUNK, G, N)
    Cc = Cm.reshape(Bsz, nc, CHUNK, G, N)
    a_cum = jnp.cumsum(a, axis=2)
    seg = a_cum[:, :, :, None] - a_cum[:, :, None, :]
    causal = (jnp.arange(CHUNK)[:, None] >= jnp.arange(CHUNK)[None, :])[None, None, :, :, None, None]
    decay = jnp.exp(jnp.where(causal, seg, -jnp.inf))
    cb = jnp.einsum('bclgn,bcsgn->bclsg', Cc, Bc)
    y_diag = jnp.einsum('bclsge,bcsgep->bclgep', cb[..., None] * decay, x)
    decay_to_end = jnp.exp(a_cum[:, :, -1:] - a_cum)
    states = jnp.einsum('bclgn,bclge,bclgep->bcgepn', Bc, decay_to_end, x)
    chunk_decay = jnp.exp(a_cum[:, :, -1])

    def chunk_step(h, inp):
        st, dec = inp
        return h * dec[..., None, None] + st, h

    h0 = jnp.zeros((Bsz, G, E, P, N), jnp.float32)
    _, prev = lax.scan(chunk_step, h0, (jnp.moveaxis(states, 1, 0), jnp.moveaxis(chunk_decay, 1, 0)))
    prev = jnp.moveaxis(prev, 0, 1)
    y_off = jnp.einsum('bclgn,bcgepn,bclge->bclgep', Cc, prev, jnp.exp(a_cum))
    return (y_diag + y_off).reshape(Bsz, T, H, P)


def mamba2_group(p, conv_w, conv_b, dt_bias, A_log, D, norm_w):
    Bsz, T, _ = p.shape
    z, xbc, dt = jnp.split(p, [C_M, C_M + CONV_DIM], axis=-1)
    xbc = jax.nn.silu(causal_depthwise_conv(xbc, conv_w, conv_b))
    xs, Bm, Cm = jnp.split(xbc, [C_M, C_M + SSM_GROUPS * SSM_STATE], axis=-1)
    dt = jax.nn.softplus(dt.astype(jnp.float32) + dt_bias.astype(jnp.float32))
    A = -jnp.exp(A_log.astype(jnp.float32))
    xh = xs.reshape(Bsz, T, H_M, HEAD_DIM).astype(jnp.float32)
    y = ssd_chunked(xh, dt, A,
                    Bm.reshape(Bsz, T, SSM_GROUPS, SSM_STATE).astype(jnp.float32),
                    Cm.reshape(Bsz, T, SSM_GROUPS, SSM_STATE).astype(jnp.float32))
    y = y + D.astype(jnp.float32)[:, None] * xh
    y = y.reshape(Bsz, T, C_M) * jax.nn.silu(z.astype(jnp.float32))
    y = y.reshape(Bsz, T, SSM_GROUPS, C_M // SSM_GROUPS)
    y = y * lax.rsqrt(jnp.mean(y * y, axis=-1, keepdims=True) + SSM_NORM_EPS)
    return (y.reshape(Bsz, T, C_M) * norm_w.astype(jnp.float32)).astype(p.dtype)


def forgetting_attention(q, k, v, log_f):
    Bsz, T, H, Dh = q.shape
    nb = T // Q_BLOCK
    scale = 1.0 / math.sqrt(Dh)
    cT = jnp.cumsum(log_f, axis=1).transpose(0, 2, 1)
    qb = jnp.moveaxis(q.reshape(Bsz, nb, Q_BLOCK, H, Dh), 1, 0)
    cb = jnp.moveaxis(cT.reshape(Bsz, H, nb, Q_BLOCK), 2, 0)
    starts = jnp.arange(nb, dtype=jnp.int32) * Q_BLOCK
    key_pos = jnp.arange(T, dtype=jnp.int32)

    def block(args):
        q_i, c_i, s0 = args
        s = jnp.einsum('bqhd,bkhd->bhqk', q_i, k) * scale
        s = s + (c_i[..., :, None] - cT[..., None, :])
        mask = (s0 + jnp.arange(Q_BLOCK, dtype=jnp.int32))[:, None] >= key_pos[None, :]
        s = jnp.where(mask[None, None], s, -jnp.inf)
        pr = jax.nn.softmax(s, axis=-1)
        return jnp.einsum('bhqk,bkhd->bqhd', pr, v)

    out = lax.map(block, (qb, cb, starts))
    return jnp.moveaxis(out, 0, 1).reshape(Bsz, T, H, Dh)


def fox_group(p, f_bias, norm_w):
    Bsz, T, _ = p.shape
    q, k, v, f = jnp.split(p, [C_F, 2 * C_F, 3 * C_F], axis=-1)

    def heads(t):
        return t.reshape(Bsz, T, H_F, HEAD_DIM).astype(jnp.float32)

    log_f = jax.nn.log_sigmoid(f.astype(jnp.float32) + f_bias.astype(jnp.float32))
    y = forgetting_attention(heads(q), heads(k), heads(v), log_f)
    y = y * lax.rsqrt(jnp.mean(y * y, axis=-1, keepdims=True) + NORM_EPS)
    return (y.reshape(Bsz, T, C_F) * norm_w.astype(jnp.float32)).astype(p.dtype)


def setup_inputs(seed: int = 0) -> dict:
    key = jax.random.key(seed)
    ks = jax.random.split(key, 28)
    L = DEPTH

    def nrm(k, shape, scale):
        return jax.random.normal(k, shape, jnp.float32) * scale

    def gain(k, shape):
        return 1.0 + 0.02 * jax.random.normal(k, shape, jnp.float32)

    def unif(k, shape, lo, hi):
        return jax.random.uniform(k, shape, jnp.float32, minval=lo, maxval=hi)

    dt0 = jnp.exp(unif(ks[17], (L, H_M), math.log(1e-3), math.log(1e-1)))
    return {
        'x': nrm(ks[0], (BATCH, SEQ, D_MODEL), 1.0),
        'norm_mix_pre': gain(ks[1], (L, D_MODEL)),
        'norm_mix_post': gain(ks[2], (L, D_MODEL)),
        'w_in': nrm(ks[3], (L, D_MODEL, D_IN), D_MODEL ** -0.5),
        'rwkv_mu': unif(ks[4], (L, RWKV_COLS), 0.0, 1.0),
        'rwkv_w0': unif(ks[5], (L, C_R), -6.0, -1.0),
        'rwkv_w2': nrm(ks[6], (L, R_DECAY, C_R), 0.1 * R_DECAY ** -0.5),
        'rwkv_a0': nrm(ks[7], (L, C_R), 0.1),
        'rwkv_a2': nrm(ks[8], (L, R_AAA, C_R), 0.5 * R_AAA ** -0.5),
        'rwkv_g2': nrm(ks[9], (L, R_GATE, C_R), R_GATE ** -0.5),
        'rwkv_k_k': 0.85 + nrm(ks[10], (L, C_R), 0.05),
        'rwkv_k_a': 1.0 + nrm(ks[11], (L, C_R), 0.05),
        'rwkv_r_k': nrm(ks[12], (L, H_R, HEAD_DIM), 0.1),
        'rwkv_ln_w': gain(ks[13], (L, C_R)),
        'rwkv_ln_b': nrm(ks[14], (L, C_R), 0.02),
        'ssm_conv_w': nrm(ks[15], (L, CONV_K, CONV_DIM), 0.5),
        'ssm_conv_b': nrm(ks[16], (L, CONV_DIM), 0.02),
        'ssm_dt_bias': dt0 + jnp.log(-jnp.expm1(-dt0)),
        'ssm_A_log': jnp.log(unif(ks[18], (L, H_M), 1.0, 16.0)),
        'ssm_D': 1.0 + nrm(ks[19], (L, H_M), 0.1),
        'ssm_norm_w': gain(ks[20], (L, C_M)),
        'fox_f_bias': unif(ks[21], (L, H_F), 1.0, 5.0),
        'fox_norm_w': gain(ks[22], (L, C_F)),
        'w_out': nrm(ks[23], (L, D_MIX, D_MODEL), D_MIX ** -0.5),
        'norm_mlp_pre': gain(ks[24], (L, D_MODEL)),
        'norm_mlp_post': gain(ks[25], (L, D_MODEL)),
        'w_mlp_up': nrm(ks[26], (L, D_MODEL, D_FF), D_MODEL ** -0.5),
        'w_mlp_down': nrm(ks[27], (L, D_FF, D_MODEL), D_FF ** -0.5),
    }


def reference(x, norm_mix_pre, norm_mix_post, w_in, rwkv_mu, rwkv_w0, rwkv_w2, rwkv_a0,
              rwkv_a2, rwkv_g2, rwkv_k_k, rwkv_k_a, rwkv_r_k, rwkv_ln_w, rwkv_ln_b,
              ssm_conv_w, ssm_conv_b, ssm_dt_bias, ssm_A_log, ssm_D, ssm_norm_w,
              fox_f_bias, fox_norm_w, w_out, norm_mlp_pre, norm_mlp_post, w_mlp_up,
              w_mlp_down):
    h = x
    for l in range(DEPTH):
        u = rms_norm(h, norm_mix_pre[l])
        proj = u @ w_in[l]
        p_r, p_m, p_f = jnp.split(proj, [RWKV_COLS, RWKV_COLS + SSM_COLS], axis=-1)
        y_r = rwkv7_group(p_r, rwkv_mu[l], rwkv_w0[l], rwkv_w2[l], rwkv_a0[l], rwkv_a2[l],
                          rwkv_g2[l], rwkv_k_k[l], rwkv_k_a[l], rwkv_r_k[l],
                          rwkv_ln_w[l], rwkv_ln_b[l])
        y_m = mamba2_group(p_m, ssm_conv_w[l], ssm_conv_b[l], ssm_dt_bias[l], ssm_A_log[l],
                           ssm_D[l], ssm_norm_w[l])
        y_f = fox_group(p_f, fox_f_bias[l], fox_norm_w[l])
        mix = jnp.concatenate([y_r, y_m, y_f], axis=-1) @ w_out[l]
        h = h + rms_norm(mix, norm_mix_post[l])
        u = rms_norm(h, norm_mlp_pre[l])
        ff = jnp.square(jax.nn.relu(u @ w_mlp_up[l])) @ w_mlp_down[l]
        h = h + rms_norm(ff, norm_mlp_post[l])
    return h
```

```python
import numpy as np
from contextlib import ExitStack
import concourse.bass as bass
import concourse.mybir as mybir

F32 = mybir.dt.float32
BF16 = mybir.dt.bfloat16
AF = mybir.ActivationFunctionType
ALU = mybir.AluOpType
AX = mybir.AxisListType

EPOCH = 30000


class Buf:
    __slots__ = ("name", "lw", "rd", "dsem", "dcnt")

    def __init__(self, name=""):
        self.name = name
        self.lw = None
        self.rd = {}
        self.dsem = None
        self.dcnt = 0


class KB:
    def __init__(self, nc):
        self.nc = nc
        self.E = {"pe": nc.tensor, "act": nc.scalar, "dve": nc.vector, "pool": nc.gpsimd, "sp": nc.sync}
        self.esems = {e: [] for e in self.E}
        self.ecnt = {e: 0 for e in self.E}
        self.seen = {e: {} for e in self.E}
        self.nsem = 0
        self.n_ins = 0
        self.n_wait = 0
        self.reg = {}
        self.dbufs = []
        self.limit = 10 ** 9
        self.uid = 0
        self.rec = None
        self.hook = None
        self.in_hook = False

    def newsem(self, name):
        self.nsem += 1
        return self.nc.alloc_semaphore(f"s{self.nsem}_{name}")

    def _wait(self, eng, tok):
        sem, val = tok
        k = id(sem)
        if self.seen[eng].get(k, 0) >= val:
            return
        self.seen[eng][k] = val
        self.E[eng].wait_ge(sem, val)
        self.n_wait += 1

    def _deps(self, eng, reads, writes):
        toks = []
        for b in reads:
            if b.lw is not None:
                toks.append(b.lw)
        for b in writes:
            if b.lw is not None:
                toks.append(b.lw)
            toks.extend(b.rd.values())
        own = self.esems[eng]
        for t in toks:
            if eng == "pe" and any(t[0] is s for s in own):
                continue
            self._wait(eng, t)

    def _commit(self, tok, reads, writes):
        k = id(tok[0])
        for b in reads:
            b.rd[k] = tok
        for b in writes:
            b.lw = tok
            b.rd = {}

    def _etok(self, eng, ins):
        c = self.ecnt[eng]
        ep, pos = divmod(c, EPOCH)
        if ep >= len(self.esems[eng]):
            self.esems[eng].append(self.newsem(f"{eng}{ep}"))
        sem = self.esems[eng][ep]
        ins.then_inc(sem, 1)
        self.ecnt[eng] = c + 1
        return (sem, pos + 1)

    def op(self, eng, fn, reads=(), writes=()):
        if self.rec is not None:
            self.rec.append((eng, fn, list(reads), list(writes)))
            return None
        if self.n_ins >= self.limit:
            return None
        self._deps(eng, reads, writes)
        ins = fn(self.E[eng])
        tok = self._etok(eng, ins)
        self._commit(tok, reads, writes)
        self.n_ins += 1
        if self.hook is not None and not self.in_hook:
            self.in_hook = True
            self.hook()
            self.in_hook = False
        return tok

    def replay(self, item):
        sv, self.in_hook = self.in_hook, True
        if item[0] == "__dma__":
            q, out, in_, reads, writes, sembuf, kw = item[1]
            self.dma(q, out, in_, reads=reads, writes=writes, sembuf=sembuf, **kw)
        else:
            self.op(*item)
        self.in_hook = sv

    def dma(self, q, out, in_, reads=(), writes=(), sembuf=None, **kw):
        if self.rec is not None:
            self.rec.append(("__dma__", (q, out, in_, list(reads), list(writes), sembuf, kw)))
            return None
        if self.n_ins >= self.limit:
            return None
        self._deps(q, reads, writes)
        sb = sembuf if sembuf is not None else (writes[0] if writes else reads[0])
        if sb.dsem is None:
            sb.dsem = self.newsem("d" + sb.name)
            self.dbufs.append(sb)
        sb.dcnt += 16
        self.E[q].dma_start(out=out, in_=in_, **kw).then_inc(sb.dsem, 16)
        tok = (sb.dsem, sb.dcnt)
        self._commit(tok, reads, writes)
        self.n_ins += 1
        return tok

    def T(self, es, name, shape, dt, psum=False):
        name = f"{name}_{self.uid}"
        t = es.enter_context((self.nc.psum_tensor if psum else self.nc.sbuf_tensor)(name, shape, dt))
        self.reg[t.name] = Buf(name)
        return t

    def bufs_of(self, kwargs, extra_r=(), extra_w=()):
        r, w = list(extra_r), list(extra_w)
        for k, v in kwargs.items():
            if type(v).__name__ == "AP":
                b = self.reg.get(v.tensor.name)
                if b is None:
                    continue
                if v.tensor.name.startswith("PS"):
                    r.append(b)
                    w.append(b)
                else:
                    (w if k in ("out", "accum_out") else r).append(b)
        return r, w

    def i(self, eng, meth, _r=(), _w=(), **kwargs):
        r, w = self.bufs_of(kwargs, _r, _w)
        return self.op(eng, lambda e: getattr(e, meth)(**kwargs), reads=r, writes=w)

    def pe(self, calls, _r=(), _w=()):
        r, w = list(_r), list(_w)
        for m, kw in calls:
            r2, w2 = self.bufs_of(kw)
            r += r2
            w += w2

        def fn(e):
            for m, kw in calls:
                ins = getattr(e, m)(**kw)
            return ins
        return self.op("pe", fn, reads=r, writes=w)

    def d(self, q, out, in_, _r=(), _w=(), **kw):
        r, w = self.bufs_of({"out": out, "in_": in_}, _r, _w)
        sb = None
        for b in w + r:
            sb = b
            break
        ob = self.reg.get(out.tensor.name)
        ib = self.reg.get(in_.tensor.name)
        sb = ob if ob is not None else (ib if ib is not None else sb)
        return self.dma(q, out, in_, reads=r, writes=w, sembuf=sb, **kw)

    def barrier(self):
        toks = []
        for e in self.E:
            c = self.ecnt[e]
            if c:
                ep, pos = divmod(c - 1, EPOCH)
                toks.append((self.esems[e][ep], pos + 1))
        for b in self.dbufs:
            toks.append((b.dsem, b.dcnt))
        for e in self.E:
            for t in toks:
                if e == "pe" and any(t[0] is s for s in self.esems[e]):
                    continue
                self._wait(e, t)

    def wait_all(self, eng, bufs):
        for b in bufs:
            if b.lw is not None:
                self._wait(eng, b.lw)
            for t in b.rd.values():
                self._wait(eng, t)


D = 1024
DFF = 4096
NORM_EPS = 1e-6


def make_consts(kb, nc):
    c = {}
    c["ident_f"] = nc.alloc_sbuf_tensor("ident_f", [128, 128], F32)
    c["ident_b"] = nc.alloc_sbuf_tensor("ident_b", [128, 128], BF16)
    b = Buf("consts")
    c["buf"] = b
    idf, idb = c["ident_f"], c["ident_b"]
    kb.op("pool", lambda e: e.memset(idf[:], 1.0), writes=[b])
    kb.op("pool", lambda e: e.affine_select(out=idf[:], in_=idf[:], pattern=[[-1, 128]],
                                            compare_op=ALU.is_equal, fill=0.0, base=0, channel_multiplier=1),
          reads=[b], writes=[b])
    kb.op("pool", lambda e: e.tensor_copy(out=idb[:], in_=idf[:]), reads=[b], writes=[b])
    return c


def rstd_from_ss(kb, ss_ap, tmp_ap, out_ap, n, eps, bufs_r, bufs_w):
    kb.op("act", lambda e: e.activation(out=tmp_ap, in_=ss_ap, func=AF.Ln, scale=1.0 / n, bias=eps),
          reads=bufs_r, writes=bufs_w)
    kb.op("act", lambda e: e.activation(out=out_ap, in_=tmp_ap, func=AF.Exp, scale=-0.5),
          reads=bufs_w, writes=bufs_w)


def mlp_phase(kb, nc, cst, h_in, h_out, hin_bufs, hout_bufs, w_up, w_dn, g_pre, g_post, ntok, TT=256, dbg=None):
    kb.uid += 1
    J = TT // 128
    NT = ntok // TT
    KC = D // 128
    FC = DFF // 128
    with ExitStack() as es:
        wup = es.enter_context(nc.sbuf_tensor(f"wup_{kb.uid}", [128, KC, DFF], BF16))
        wdn = es.enter_context(nc.sbuf_tensor(f"wdn_{kb.uid}", [128, FC, D], BF16))
        gpre = es.enter_context(nc.sbuf_tensor(f"gpre_{kb.uid}", [128, KC], F32))
        gpost = es.enter_context(nc.sbuf_tensor(f"gpost_{kb.uid}", [128, D], F32))
        b_w = Buf("mlpw")
        b_g = Buf("mlpg")
        kb.dma("sp", gpre[:], g_pre.rearrange("(c p) -> p c", p=128), writes=[b_g], allow_slow_non_contiguous=True)
        kb.dma("sp", gpost[:], g_post.rearrange("(o n) -> o n", o=1).to_broadcast([128, D]), writes=[b_g])
        hb0 = es.enter_context(nc.sbuf_tensor(f"hb0_{kb.uid}", [128, J, D], F32))
        hb1 = es.enter_context(nc.sbuf_tensor(f"hb1_{kb.uid}", [128, J, D], F32))
        u = es.enter_context(nc.sbuf_tensor(f"u_{kb.uid}", [128, J, D], BF16))
        uT0 = es.enter_context(nc.sbuf_tensor(f"uT0_{kb.uid}", [128, KC, TT], BF16))
        uT1 = es.enter_context(nc.sbuf_tensor(f"uT1_{kb.uid}", [128, KC, TT], BF16))
        hT = es.enter_context(nc.sbuf_tensor(f"hT_{kb.uid}", [128, FC, TT], BF16))
        r0 = es.enter_context(nc.sbuf_tensor(f"r0_{kb.uid}", [128, TT], F32))
        r1 = es.enter_context(nc.sbuf_tensor(f"r1_{kb.uid}", [128, TT], F32))
        tt = es.enter_context(nc.sbuf_tensor(f"tt_{kb.uid}", [128, D], F32))
        st = es.enter_context(nc.sbuf_tensor(f"st_{kb.uid}", [128, 16], F32))
        psT0 = es.enter_context(nc.psum_tensor(f"psT0_{kb.uid}", [128, KC, 128], BF16))
        psT1 = es.enter_context(nc.psum_tensor(f"psT1_{kb.uid}", [128, KC, 128], BF16))
        psH0 = es.enter_context(nc.psum_tensor(f"psH0_{kb.uid}", [128, 512], F32))
        psH1 = es.enter_context(nc.psum_tensor(f"psH1_{kb.uid}", [128, 512], F32))
        psO0 = es.enter_context(nc.psum_tensor(f"psO0_{kb.uid}", [128, D], F32))
        psO1 = es.enter_context(nc.psum_tensor(f"psO1_{kb.uid}", [128, D], F32))
        hb = [hb0, hb1]
        uT = [uT0, uT1]
        rr = [r0, r1]
        psT = [psT0, psT1]
        psH = [psH0, psH1]
        psO = [psO0, psO1]
        b_hb = [[Buf(f"hb{s}{j}") for j in range(J)] for s in range(2)]
        b_u = [Buf(f"u{j}") for j in range(J)]
        b_uT = [Buf("uT0"), Buf("uT1")]
        b_hT = [Buf(f"hT{f}") for f in range(FC)]
        b_r = [Buf("r0"), Buf("r1")]
        b_tt = Buf("tt")
        b_st = [Buf("st0"), Buf("st1")]
        b_psT = [Buf("psT0"), Buf("psT1")]
        b_psH = [Buf("psH0"), Buf("psH1")]
        b_psO = [Buf("psO0"), Buf("psO1")]
        identb = cst["ident_b"]
        b_c = cst["buf"]

        if True:
            assert J == 2
            stg = [hb0[:].rearrange("p j d -> p (j d)"), hb1[:].rearrange("p j d -> p (j d)")]
            b_s = [b_hb[0], b_hb[1]]
            n = 0
            for kc in range(KC):
                for hf in range(2):
                    s = n % 2
                    kb.dma("sp", stg[s], w_up[kc * 128:(kc + 1) * 128, hf * 2048:(hf + 1) * 2048], writes=b_s[s])
                    dst = wup[:, kc, hf * 2048:(hf + 1) * 2048]
                    if n % 2 == 0:
                        kb.op("act", lambda e: e.activation(out=dst, in_=stg[s], func=AF.Copy,
                                                            scale=gpre[:, kc:kc + 1]),
                              reads=b_s[s] + [b_g], writes=[b_w])
                    else:
                        kb.op("dve", lambda e: e.tensor_scalar(out=dst, in0=stg[s], scalar1=gpre[:, kc:kc + 1],
                                                               scalar2=None, op0=ALU.mult),
                              reads=b_s[s] + [b_g], writes=[b_w])
                    n += 1
            for fc in range(0, FC, 2):
                s = n % 2
                kb.dma("sp", stg[s].rearrange("p (c n) -> p c n", c=2),
                       w_dn[fc * 128:(fc + 2) * 128, :].rearrange("(c p) n -> p c n", p=128), writes=b_s[s])
                dst = wdn[:, fc:fc + 2, :]
                src = stg[s].rearrange("p (c n) -> p c n", c=2)
                eng = ["act", "dve", "pool"][(fc // 2) % 3]
                if eng == "act":
                    kb.op("act", lambda e: e.copy(out=dst, in_=src), reads=b_s[s], writes=[b_w])
                else:
                    kb.op(eng, lambda e: e.tensor_copy(out=dst, in_=src), reads=b_s[s], writes=[b_w])
                n += 1
        def load(i):
            s = i % 2
            for j in range(J):
                r0_ = i * TT + j * 128
                kb.dma("sp", hb[s][:, j, :], h_in[r0_:r0_ + 128, :],
                       reads=[hin_bufs[r0_ // 128]], writes=[b_hb[s][j]], sembuf=b_hb[s][j])

        nps = [0]

        def prenorm(i):
            s = i % 2
            stt = st[:, (s * 8):(s * 8 + 8)]
            for j in range(J):
                kb.op("act", lambda e: e.activation(out=u[:, j, :], in_=hb[s][:, j, :], func=AF.Square,
                                                    accum_out=stt[:, j:j + 1]),
                      reads=[b_hb[s][j]], writes=[b_u[j], b_st[s]])
            rstd_from_ss(kb, stt[:, 0:J], stt[:, 2:2 + J], stt[:, 4:4 + J], D, NORM_EPS, [b_st[s]], [b_st[s]])
            for j in range(J):
                kb.op("dve", lambda e: e.tensor_scalar(out=u[:, j, :], in0=hb[s][:, j, :],
                                                       scalar1=stt[:, 4 + j:5 + j], scalar2=None, op0=ALU.mult),
                      reads=[b_hb[s][j], b_st[s]], writes=[b_u[j]])
                p = nps[0] % 2
                nps[0] += 1

                def tr(e):
                    for kc in range(KC):
                        ins = e.transpose(out=psT[p][:, kc, :], in_=u[:, j, kc * 128:(kc + 1) * 128],
                                          identity=identb[:])
                    return ins
                kb.op("pe", tr, reads=[b_u[j], b_c], writes=[b_psT[p]])
                kb.op("dve", lambda e: e.tensor_copy(out=uT[s][:, :, j * 128:(j + 1) * 128], in_=psT[p][:]),
                      reads=[b_psT[p]], writes=[b_uT[s]])

        load(0)
        if NT > 1:
            load(1)
        prenorm(0)
        for i in range(NT):
            s = i % 2
            stt = st[:, (s * 8):(s * 8 + 8)]
            for f in range(FC):
                p = f % 2

                def up(e):
                    for kc in range(KC):
                        ins = e.matmul(psH[p][:, 0:TT], lhsT=wup[:, kc, f * 128:(f + 1) * 128], rhs=uT[s][:, kc, :],
                                       start=(kc == 0), stop=(kc == KC - 1))
                    return ins
                kb.op("pe", up, reads=[b_w, b_uT[s]], writes=[b_psH[p]])
                kb.op("act", lambda e: e.activation(out=rr[p][:], in_=psH[p][:, 0:TT], func=AF.Relu),
                      reads=[b_psH[p]], writes=[b_r[p]])
                kb.op("dve" if p == 0 else "pool",
                      lambda e: e.tensor_tensor(out=hT[:, f, :], in0=rr[p][:], in1=rr[p][:], op=ALU.mult),
                      reads=[b_r[p]], writes=[b_hT[f]])
            if dbg is not None and i == 0:
                kb.dma("sp", dbg["u"], u[:], reads=b_u, sembuf=b_u[0])
                kb.dma("sp", dbg["uT"], uT[s][:], reads=[b_uT[s]], sembuf=b_uT[s])
                kb.dma("sp", dbg["hT"], hT[:], reads=b_hT, sembuf=b_hT[0])
                kb.dma("sp", dbg["st"], st[:], reads=b_st, sembuf=b_st[0])
                kb.dma("sp", dbg["wup"], wup[:, 0, :], reads=[b_w], sembuf=b_w)
                kb.dma("sp", dbg["idb"], identb[:], reads=[b_c], sembuf=b_c)
                kb.dma("sp", dbg["wdn"], wdn[:], reads=[b_w], sembuf=b_w)
            if i + 1 < NT:
                prenorm(i + 1)
            for j in range(J):
                p = j % 2
                for dn in range(2):
                    def down(e):
                        for f in range(FC):
                            ins = e.matmul(psO[p][:, dn * 512:(dn + 1) * 512], lhsT=hT[:, f, j * 128:(j + 1) * 128],
                                           rhs=wdn[:, f, dn * 512:(dn + 1) * 512],
                                           start=(f == 0), stop=(f == FC - 1))
                        return ins
                    kb.op("pe", down, reads=[b_w] + b_hT, writes=[b_psO[p]])
                kb.op("act", lambda e: e.activation(out=u[:, j, :], in_=psO[p][:], func=AF.Square,
                                                    accum_out=stt[:, 6:7]),
                      reads=[b_psO[p]], writes=[b_u[j], b_st[s]])
                rstd_from_ss(kb, stt[:, 6:7], stt[:, 7:8], stt[:, 6:7], D, NORM_EPS, [b_st[s]], [b_st[s]])
                kb.op("dve", lambda e: e.scalar_tensor_tensor(out=tt[:], in0=psO[p][:], scalar=stt[:, 6:7],
                                                              in1=gpost[:], op0=ALU.mult, op1=ALU.mult),
                      reads=[b_psO[p], b_st[s], b_g], writes=[b_tt])
                kb.op("pool", lambda e: e.tensor_tensor(out=hb[s][:, j, :], in0=tt[:], in1=hb[s][:, j, :], op=ALU.add),
                      reads=[b_tt, b_hb[s][j]], writes=[b_hb[s][j]])
                r0_ = i * TT + j * 128
                kb.dma("sp", h_out[r0_:r0_ + 128, :], hb[s][:, j, :],
                       reads=[b_hb[s][j]], writes=[hout_bufs[r0_ // 128]], sembuf=b_hb[s][j])
            if i + 2 < NT:
                load(i + 2)


C0 = 0.6065306597126334
RWKV_GN_EPS = 64e-5
SSM_NORM_EPS = 1e-5
NCOL = 3212
import os as _os0
IDT = F32
NCOLP = 3232


def mix_phase(kb, nc, cst, h_in, h_out, hin_bufs, hout_bufs, P, l, nseq, T, dbg=None, upto=99):
    NB = T // 128
    KC = 8
    kb.uid += 1
    with ExitStack() as es:
        def sb(name, shape, dt=F32):
            return kb.T(es, name, shape, dt)

        Win = sb("Win", [128, KC, NCOLP], BF16)
        WinB = sb("WinB", [128, KC, 896], BF16)
        Wout = sb("Wout", [128, KC, D], BF16)
        gpre = sb("gpre", [128, KC])
        gpost = sb("gpostm", [128, D])
        rs = sb("rs", [128, KC])
        pbc = sb("pbc", [128, 7, 256])
        lr2 = sb("lr2", [128, 256])
        cw = sb("cw", [128, 8, 4])
        cb = sb("cb", [128, 8])
        sp8 = sb("sp8", [128, 4, 8])
        fb4 = sb("fb4", [128, 4])
        kb.d("sp", gpre[:], P["norm_mix_pre"][l].rearrange("(c p) -> p c", p=128), allow_slow_non_contiguous=True)
        kb.d("sp", gpost[:], P["norm_mix_post"][l:l + 1, :].to_broadcast([128, D]))
        for n_, nm in enumerate(["rwkv_w0", "rwkv_a0", "rwkv_k_k", "rwkv_k_a"]):
            kb.d("sp", pbc[:, n_, :], P[nm][l:l + 1, :].to_broadcast([128, 256]))
        kb.d("sp", pbc[:, 4, :], P["rwkv_r_k"][l:l + 1].rearrange("o h n -> o (h n)").to_broadcast([128, 256]))
        kb.d("sp", pbc[:, 5, :], P["rwkv_ln_w"][l:l + 1, :].to_broadcast([128, 256]))
        kb.d("sp", pbc[:, 6, :], P["rwkv_ln_b"][l:l + 1, :].to_broadcast([128, 256]))
        kb.d("sp", lr2[0:32, :], P["rwkv_w2"][l])
        kb.d("sp", lr2[32:64, :], P["rwkv_a2"][l])
        kb.d("sp", lr2[64:128, :], P["rwkv_g2"][l])
        for k_ in range(4):
            kb.d("sp", cw[:, :, k_], P["ssm_conv_w"][l, k_].rearrange("(c p) -> p c", p=128),
                 allow_slow_non_contiguous=True)
        kb.d("sp", cb[:], P["ssm_conv_b"][l].rearrange("(c p) -> p c", p=128), allow_slow_non_contiguous=True)
        kb.d("sp", sp8[:, 0, :], P["ssm_dt_bias"][l:l + 1, :].to_broadcast([128, 8]))
        kb.d("sp", sp8[:, 1, :], P["ssm_A_log"][l:l + 1, :].to_broadcast([128, 8]))
        kb.d("sp", sp8[:, 2, :], P["ssm_D"][l:l + 1, :].to_broadcast([128, 8]))
        kb.d("sp", fb4[:], P["fox_f_bias"][l:l + 1, :].to_broadcast([128, 4]))
        kb.i("pool", "memset", ap=rs[:], constant=1.0, _w=[kb.reg[rs.name]])
        kb.d("sp", rs[:, 2:6], P["ssm_norm_w"][l].rearrange("(c p) -> p c", p=128), allow_slow_non_contiguous=True)
        kb.d("sp", rs[:, 6:8], P["fox_norm_w"][l].rearrange("(c p) -> p c", p=128), allow_slow_non_contiguous=True)
        kb.i("act", "activation", out=sp8[:, 1, :], in_=sp8[:, 1, :], func=AF.Exp)
        kb.i("dve", "tensor_scalar", out=sp8[:, 1, :], in0=sp8[:, 1, :], scalar1=-1.0, scalar2=None, op0=ALU.mult)

        with ExitStack() as es2:
            stg = [kb.T(es2, "stgA", [128, NCOL], F32), kb.T(es2, "stgB", [128, NCOL], F32)]
            mu_bc = kb.T(es2, "mu_bc", [128, 896], F32)
            omu_bc = kb.T(es2, "omu_bc", [128, 896], F32)
            kb.d("sp", mu_bc[:], P["rwkv_mu"][l:l + 1, :].to_broadcast([128, 896]))
            kb.i("dve", "tensor_scalar", out=omu_bc[:], in0=mu_bc[:], scalar1=-1.0, scalar2=1.0, op0=ALU.mult, op1=ALU.add)
            n = 0
            for kc in range(KC):
                s_ = stg[n % 2]
                n += 1
                kb.d("sp", s_[:], P["w_in"][l, kc * 128:(kc + 1) * 128, :])
                g_ = gpre[:, kc:kc + 1]
                kb.i("dve", "scalar_tensor_tensor", out=Win[:, kc, 0:896], in0=s_[:, 0:896], scalar=g_,
                     in1=omu_bc[:], op0=ALU.mult, op1=ALU.mult)
                kb.i("dve", "scalar_tensor_tensor", out=WinB[:, kc, 0:896], in0=s_[:, 0:896], scalar=g_,
                     in1=mu_bc[:], op0=ALU.mult, op1=ALU.mult)
                kb.i("act", "activation", out=Win[:, kc, 896:2432], in_=s_[:, 896:2432], func=AF.Copy, scale=g_)
                kb.i("pool", "tensor_scalar", out=Win[:, kc, 2432:3204], in0=s_[:, 2440:3212], scalar1=g_,
                     scalar2=None, op0=ALU.mult)
                kb.i("pool", "tensor_scalar", out=Win[:, kc, 3204:3212], in0=s_[:, 2432:2440], scalar1=g_,
                     scalar2=None, op0=ALU.mult)
            for c in range(KC):
                s_ = stg[n % 2]
                n += 1
                kb.d("sp", s_[:, 0:D], P["w_out"][l, c * 128:(c + 1) * 128, :])
                if c % 2:
                    kb.i("act", "activation", out=Wout[:, c, :], in_=s_[:, 0:D], func=AF.Copy, scale=rs[:, c:c + 1])
                else:
                    kb.i("dve", "tensor_scalar", out=Wout[:, c, :], in0=s_[:, 0:D], scalar1=rs[:, c:c + 1],
                         scalar2=None, op0=ALU.mult)
            kb.barrier()
        kb.barrier()

        identf, identb = cst["ident_f"], cst["ident_b"]
        b_c = cst["buf"]
        triI = sb("triI", [128, 128])
        maskA = sb("maskA", [128, 384])
        maskB = sb("maskB", [128, 256])
        ones = sb("ones", [128, 128])
        halfm = sb("halfm", [128, 128])
        NEGf = sb("NEGf", [128, 128])
        NEGb = sb("NEGb", [128, 128], BF16)
        kb.i("pool", "memset", ap=ones[:], constant=1.0, _w=[kb.reg[ones.name]])
        kb.i("pool", "memset", ap=halfm[:], constant=0.0, _w=[kb.reg[halfm.name]])
        kb.i("pool", "memset", ap=halfm[0:64, :], constant=1.0, _w=[kb.reg[halfm.name]])

        def asel(out_ap, in_ap, pat, cm, op, fill):
            kb.i("pool", "affine_select", out=out_ap, in_=in_ap, pattern=pat, compare_op=op, fill=fill, base=0,
                 channel_multiplier=cm)
        asel(triI[:], ones[:], [[1, 128]], -1, ALU.is_ge, 0.0)
        asel(maskA[:, 0:128], ones[:], [[1, 128]], -1, ALU.is_gt, 0.0)
        kb.i("pool", "tensor_copy", out=maskA[:, 128:256], in_=triI[:])
        kb.i("pool", "tensor_copy", out=maskA[:, 256:384], in_=triI[:])
        asel(maskB[:, 0:128], ones[:], [[-1, 128]], 1, ALU.is_gt, 0.0)
        kb.i("pool", "tensor_copy", out=maskB[:, 128:256], in_=maskB[:, 0:128])
        kb.i("pool", "tensor_scalar", out=NEGf[:], in0=maskB[:, 0:128], scalar1=-1.0e4, scalar2=None, op0=ALU.mult)
        kb.i("pool", "tensor_scalar", out=NEGb[:], in0=maskB[:, 0:128], scalar1=-30000.0, scalar2=None, op0=ALU.mult)

        hb = [sb("mhb0", [128, D]), sb("mhb1", [128, D])]
        u = sb("mu_", [128, D], BF16)
        uT = sb("muT0", [128, KC, 128], BF16)
        uTp = sb("muTp", [128, KC, 128], BF16)
        st = sb("mst", [128, 32])
        rk = sb("rk", [128, 512])
        v_tm = sb("v_tm", [128, 256])
        zs = sb("zs", [128, 512])
        lr_fm = sb("lr_fm", [128, 128])
        xraw = sb("xraw0", [128, 8, 132])
        xcar = sb("xcar", [128, 8, 4])
        xacc = sb("xacc", [128, 8, 128])
        xbcT = sb("xbcT", [128, 8, 128], BF16)
        x_tm = sb("x_tm", [128, 512], BF16)
        B_tm = sb("B_tm", [128, 256], BF16)
        qT = sb("qT", [128, 2, 128], BF16)
        KT = [sb(f"KT{j}", [128, 2, 128], BF16) for j in range(NB)]
        VA = [sb(f"VA{j}", [128, 4, 66], BF16) for j in range(NB)]
        tok8 = sb("tok8", [128, 16])
        R = {nm: sb("rw_" + nm, [128, 256]) for nm in
             ["sg", "a", "g", "kk", "kkn", "km", "b", "cs", "t1", "t2", "E", "E2", "E3", "rt", "kt", "bt", "at", "Kd", "Bd",
              ]}
        R["ysb"] = R["kt"]
        R["yc"] = R["bt"]
        st4 = sb("st4", [128, 32])
        FMa = sb("FMa", [128, 2, 2, 128])
        FMb = sb("FMb", [128, 2, 2, 128])
        SA = [sb(f"SA{h}", [128, 256]) for h in range(4)]
        SBt = [sb(f"SBt{h}", [128, 128], IDT) for h in range(4)]
        Wv = [sb(f"Wv{h}", [128, 384], IDT) for h in range(4)]
        Mk2 = [sb(f"Mk2{h}", [128, 128]) for h in range(4)]
        Wt = sb("Wt", [128, 2, 128])
        U_tm = sb("U_tm", [128, 256])
        ST = sb("STs", [128, 2, 64])
        GC = sb("GC", [128, 2])
        dts = sb("dts", [128, 8, 8])
        dAb = sb("dAb", [128, 8, 128])
        dec = [sb("dec0", [128, 128]), sb("dec1", [128, 128])]
        Lh = sb("Lh", [128, 8, 128], BF16)
        xdt = sb("xdt", [128, 512], BF16)
        xdte = sb("xdte", [128, 512], BF16)
        hst = sb("hst", [128, 512])
        hbf = sb("hbf", [128, 512], BF16)
        Et = [sb("Et0", [128, 4, 128], BF16), sb("Et1", [128, 4, 128], BF16)]
        ckg = sb("ckg", [128, NB, 4])
        ball = sb("ball", [128, NB, 4])
        prefix = sb("prefix", [128, 4])
        refn = sb("refn", [128, 4])
        mix_tm = sb("mix_tm", [128, D], BF16)
        mixT = sb("mixT", [128, KC, 128], BF16)
        PS = [kb.T(es, f"PS{i}", [128, 512], F32, psum=True) for i in range(8)]

        class _V:
            def __init__(self, ap):
                self.ap = ap

            def __getitem__(self, k):
                return self.ap if k == slice(None) else self.ap[k]
        tt = _V(xacc[:].rearrange("p c t -> p (c t)"))
        ya = _V(xacc[:, 0:4, :].rearrange("p c t -> p (c t)"))
        yb = _V(xacc[:, 4:8, :].rearrange("p c t -> p (c t)"))
        fo = _V(xacc[:, 4:6, :].rearrange("p c (a d) -> p (c a) d", a=2))
        fo2 = _V(xacc[:, 6:8, :].rearrange("p c (a d) -> p (c a) d", a=2))

        def bf(ps_ap):
            return ps_ap.bitcast(BF16)

        def bc(ap, shape):
            return ap.unsqueeze(len(ap.shape)).to_broadcast(shape)

        cnt = [0]

        def evac_eng():
            cnt[0] += 1
            return "act" if cnt[0] % 2 else "dve"

        def copy(eng, out, in_):
            if eng == "act":
                kb.i("act", "activation", out=out, in_=in_, func=AF.Copy)
            else:
                kb.i(eng, "tensor_copy", out=out, in_=in_)

        def rstd(ss_ap, tmp_ap, out_ap, n, eps):
            kb.i("act", "activation", out=tmp_ap, in_=ss_ap, func=AF.Ln, scale=1.0 / n, bias=eps)
            kb.i("act", "activation", out=out_ap, in_=tmp_ap, func=AF.Exp, scale=-0.5)

        if upto < 7:
            kb.i("pool", "memset", ap=mix_tm[:], constant=0.0, _w=[kb.reg[mix_tm.name]])
        QF = []
        a_pre = [False]

        def qf_hook():
            if QF:
                kb.replay(QF.pop(0))
        for seq in range(nseq):
            kb.i("pool", "memset", ap=ST[:], constant=0.0, _w=[kb.reg[ST.name]])
            kb.i("pool", "memset", ap=hst[:], constant=0.0, _w=[kb.reg[hst.name]])
            kb.i("pool", "memset", ap=hbf[:], constant=0.0, _w=[kb.reg[hbf.name]])
            kb.i("pool", "memset", ap=prefix[:], constant=0.0, _w=[kb.reg[prefix.name]])
            kb.i("pool", "memset", ap=uT[:], constant=0.0, _w=[kb.reg[uT.name]])
            kb.i("pool", "memset", ap=uTp[:], constant=0.0, _w=[kb.reg[uTp.name]])
            kb.i("pool", "memset", ap=xraw[:], constant=0.0, _w=[kb.reg[xraw.name]])
            kb.i("pool", "memset", ap=xcar[:], constant=0.0, _w=[kb.reg[xcar.name]])
            kb.d("sp", hb[0][:], h_in[seq * T:seq * T + 128, :], _r=[hin_bufs[seq * NB]])
            for blk in range(NB):
                s = blk % 2
                row0 = seq * T + blk * 128
                hbs = hb[s]
                kb.hook = qf_hook
                uTs = uT
                def stage_A(hbx, has_prev):
                    kb.i("act", "activation", out=u[:], in_=hbx[:], func=AF.Square, accum_out=st[:, 0:1])
                    rstd(st[:, 0:1], st[:, 1:2], st[:, 2:3], D, NORM_EPS)
                    kb.i("dve", "tensor_scalar", out=u[:], in0=hbx[:], scalar1=st[:, 2:3], scalar2=None, op0=ALU.mult)
                    psT_ = bf(PS[0][:]).rearrange("p (c t) -> p c t", c=8)
                    kb.pe([("transpose", dict(out=psT_[:, kc, :], in_=u[:, kc * 128:(kc + 1) * 128], identity=identb[:]))
                           for kc in range(KC)], _r=[b_c])
                    if has_prev:
                        kb.i("act", "activation", out=uTp[:, :, 0:1], in_=uT[:, :, 127:128], func=AF.Copy)
                    kb.i("dve", "tensor_copy", out=uT[:], in_=psT_)
                    kb.i("dve", "tensor_copy", out=uTp[:, :, 1:128], in_=psT_[:, :, 0:127])
                if not a_pre[0]:
                    stage_A(hbs, blk > 0)
                a_pre[0] = False
                import os as _os
                cur = (lambda kc: mixT[:, kc, :]) if _os.environ.get("VAR4") == "m" else (lambda kc: uTs[:, kc, :])
                prv = lambda kc: uTp[:, kc, :]

                if upto >= 2:
                    import os as _os

                    def proj_tm(ps_ap, c0, c1, shift):
                        calls = []
                        nmm = KC * (2 if shift else 1)
                        k_ = 0
                        for kc in range(KC):
                            calls.append(("matmul", dict(out=ps_ap, lhsT=cur(kc), rhs=(Wout[:, kc, 0:c1 - c0] if _os.environ.get("VAR2") == "w" else Win[:, kc, c0:c1]),
                                                         start=(k_ == 0), stop=(k_ == nmm - 1))))
                            k_ += 1
                            if shift:
                                calls.append(("matmul", dict(out=ps_ap, lhsT=prv(kc), rhs=WinB[:, kc, c0:c1],
                                                             start=False, stop=(k_ == nmm - 1))))
                                k_ += 1
                        kb.pe(calls)

                    def proj_fm(ps_ap, c0, shift=False):
                        calls = []
                        nmm = KC * (2 if shift else 1)
                        k_ = 0
                        for kc in range(KC):
                            calls.append(("matmul", dict(out=ps_ap, lhsT=Win[:, kc, c0:c0 + 128], rhs=cur(kc),
                                                         start=(k_ == 0), stop=(k_ == nmm - 1))))
                            k_ += 1
                            if shift:
                                calls.append(("matmul", dict(out=ps_ap, lhsT=WinB[:, kc, c0:c0 + 128],
                                                             rhs=prv(kc), start=False, stop=(k_ == nmm - 1))))
                                k_ += 1
                        kb.pe(calls)

                    if upto >= 2.1:
                        import os as _os
                        _v = _os.environ.get("VAR", "")
                        _pp = PS[1][:, 0:512]
                        proj_tm(_pp, 0, 512, upto != 2.1)
                        if _v != "a":
                            copy("dve" if _v == "b" else "act", (tt[:, 0:512] if _os.environ.get("VAR5") == "t" else rk[:]), _pp)
                        if _v != "c":
                            proj_tm(PS[2][:, 0:256], 512, 768, upto != 2.1)
                            copy("dve", v_tm[:], PS[2][:, 0:256])
                    if upto >= 2.2:
                        proj_tm(PS[1][:, 0:512], 896, 1408, False)
                        kb.i("act", "activation", out=zs[:], in_=PS[1][:, 0:512], func=AF.Silu)
                    if upto >= 2.3:
                        proj_tm(PS[2][:, 0:268], 2944, 3212, False)
                        kb.i("dve", "tensor_copy", out=VA[blk][:, :, 0:64],
                             in_=PS[2][:, 0:256].rearrange("p (h d) -> p h d", h=4))
                        kb.i("pool", "memset", ap=VA[blk][:, :, 64:65], constant=1.0, _w=[kb.reg[VA[blk].name]])
                        kb.i("dve", "tensor_copy", out=tok8[:, 0:12], in_=PS[2][:, 256:268])
                    if upto >= 2.4:
                        proj_fm(PS[3][:, 0:128], 768, True)
                        kb.i("act", "activation", out=lr_fm[0:32, :], in_=PS[3][0:32, 0:128], func=AF.Tanh)
                        kb.i("act", "activation", out=lr_fm[32:64, :], in_=PS[3][32:64, 0:128], func=AF.Copy)
                        kb.i("act", "activation", out=lr_fm[64:128, :], in_=PS[3][64:128, 0:128], func=AF.Sigmoid)
                    xr = xraw
                    if upto >= 2.5:
                        kb.i("dve", "tensor_copy", out=xr[:, :, 0:3], in_=xcar[:, :, 0:3])
                    if upto >= 2.5:
                        for c4 in range(2):
                            pst = PS[4 + c4]
                            for c in range(4):
                                proj_fm(pst[:, c * 128:(c + 1) * 128], 1408 + (c4 * 4 + c) * 128)
                            copy("act" if c4 else "dve", xr[:, c4 * 4:(c4 + 1) * 4, 3:131],
                                 pst[:].rearrange("p (c t) -> p c t", c=4))
                    if upto >= 2.6:
                        for c in range(4):
                            proj_fm(PS[3][:, c * 128:(c + 1) * 128], 2432 + c * 128)
                        copy("act", qT[:], PS[3][:, 0:256].rearrange("p (c t) -> p c t", c=2))
                        copy("dve", KT[blk][:], PS[3][:, 256:512].rearrange("p (c t) -> p c t", c=2))

                kb.hook = None
                while QF:
                    kb.replay(QF.pop(0))
                if blk + 1 < NB:
                    kb.d("sp", hb[1 - s][:], h_in[row0 + 128:row0 + 256, :], _r=[hin_bufs[seq * NB + blk + 1]])
                if upto >= 6:
                    TT0 = lambda e_, o_, a_, b_, op_: kb.i(e_, "tensor_tensor", out=o_, in0=a_, in1=b_, op=op_)
                    mmf0 = lambda **kw: ("matmul", dict(start=True, stop=True, **kw))
                    TT0("dve", tok8[:, 12:16], tok8[:, 0:4], fb4[:], ALU.add)
                    kb.i("act", "activation", out=tok8[:, 12:16], in_=tok8[:, 12:16], func=AF.Exp, scale=-1.0)
                    kb.i("act", "activation", out=tok8[:, 12:16], in_=tok8[:, 12:16], func=AF.Ln, bias=1.0)
                    kb.pe([mmf0(out=PS[5][:, 16:20], lhsT=triI[:], rhs=tok8[:, 12:16]),
                           mmf0(out=PS[5][:, 20:24], lhsT=ones[:], rhs=tok8[:, 12:16]),
                           mmf0(out=PS[5][:, 24:28], lhsT=halfm[:], rhs=tok8[:, 12:16])])
                    TT0("dve", ckg[:, blk, :], PS[5][:, 16:20], prefix[:], ALU.add)
                    TT0("dve", refn[:], PS[5][:, 24:28], prefix[:], ALU.add)
                    TT0("dve", prefix[:], PS[5][:, 20:24], prefix[:], ALU.add)
                    TT0("dve", ball[:, 0:blk + 1, :], ckg[:, 0:blk + 1, :],
                        refn[:].unsqueeze(1).to_broadcast([128, blk + 1, 4]), ALU.subtract)
                fox_groups = []
                if upto >= 6:
                    for h in range(4):
                        for j0 in range(0, blk + 1, 4):
                            fox_groups.append((h, list(range(j0, min(j0 + 4, blk + 1)))))
                psS2 = [PS[0], PS[0]]

                def fox_S(k):
                    h, js = fox_groups[k]
                    hp, q_ = h // 2, h % 2
                    pr = slice(64 * q_, 64 * q_ + 64)
                    psS = psS2[k % 2]
                    calls = []
                    for jj, j in enumerate(js):
                        calls.append(("matmul", dict(out=psS[:, jj * 128:(jj + 1) * 128], lhsT=KT[j][pr, hp, :],
                                                     rhs=qT[pr, hp, :], start=True, stop=(j != blk))))
                        if j == blk:
                            calls.append(("matmul", dict(out=psS[:, jj * 128:(jj + 1) * 128], lhsT=identb[:],
                                                         rhs=NEGb[:], start=False, stop=True)))
                    kb.pe(calls, _r=[b_c])

                def fox_E(k):
                    h, js = fox_groups[k]
                    psS = psS2[k % 2]
                    for jj, j in enumerate(js):
                        kb.i("act", "activation", out=Et[k % 2][:, jj, :], in_=psS[:, jj * 128:(jj + 1) * 128],
                             func=AF.Exp, scale=0.125, bias=ball[:, j, h:h + 1])

                def fox_PV(k):
                    h, js = fox_groups[k]
                    kb.pe([("matmul", dict(out=PS[5][:, h * 128:h * 128 + 65], lhsT=Et[k % 2][:, jj, :],
                                           rhs=VA[j][:, h, 0:65], start=(j == 0), stop=(j == blk)))
                           for jj, j in enumerate(js)])
                fox_state = [0]
                n_fg = len(fox_groups)
                if n_fg:
                    fox_S(0)
                n_fox_pts = 15
                fox_per = -(-(n_fg + 2) // n_fox_pts) if n_fg else 0

                def fox_pull(n=None):
                    n = fox_per if n is None else n
                    for _ in range(n):
                        k = fox_state[0]
                        if k >= n_fg + 1 or n_fg == 0:
                            return
                        if k < n_fg:
                            fox_E(k)
                        if k + 1 < n_fg:
                            fox_S(k + 1)
                        if k >= 1:
                            fox_PV(k - 1)
                        fox_state[0] = k + 1

                def ssd_gen():
                    if upto >= 3:
                        for c in range(8):
                            kb.i("dve", "tensor_scalar", out=xacc[:, c, :], in0=xr[:, c, 0:128], scalar1=cw[:, c, 0:1],
                                 scalar2=cb[:, c:c + 1], op0=ALU.mult, op1=ALU.add)
                            for k_ in range(1, 4):
                                kb.i("dve", "scalar_tensor_tensor", out=xacc[:, c, :], in0=xr[:, c, k_:k_ + 128],
                                     scalar=cw[:, c, k_:k_ + 1], in1=xacc[:, c, :], op0=ALU.mult, op1=ALU.add)
                        kb.i("dve", "tensor_copy", out=xcar[:, :, 0:3], in_=xr[:, :, 128:131])
                        kb.i("act", "activation", out=xbcT[:], in_=xacc[:], func=AF.Silu)
                        psT = bf(PS[6][:]).rearrange("p (c t) -> p c t", c=8)
                        kb.pe([("transpose", dict(out=psT[:, c, :], in_=xbcT[:, c, :], identity=identb[:])) for c in range(6)],
                              _r=[b_c])
                        kb.i("dve", "tensor_copy", out=x_tm[:], in_=bf(PS[6][:])[:, 0:512])
                        kb.i("act", "activation", out=B_tm[:], in_=bf(PS[6][:])[:, 512:768], func=AF.Copy)
                    if upto < 5:
                        return
                    TT_ = lambda e_, o_, a_, b_, op_: kb.i(e_, "tensor_tensor", out=o_, in0=a_, in1=b_, op=op_)
                    mmf = lambda **kw: ("matmul", dict(start=True, stop=True, **kw))
                    h8 = lambda ap: ap.rearrange("p (h d) -> p h d", h=8)
                    TT_("dve", dts[:, 0, :], tok8[:, 4:12], sp8[:, 0, :], ALU.add)
                    kb.i("act", "activation", out=dts[:, 7, :], in_=dts[:, 0, :], func=AF.Exp)
                    kb.i("act", "activation", out=dts[:, 0, :], in_=dts[:, 7, :], func=AF.Ln, bias=1.0)
                    TT_("dve", dts[:, 1, :], dts[:, 0, :], sp8[:, 1, :], ALU.mult)
                    yield
                    kb.pe([mmf(out=PS[7][:, 0:8], lhsT=triI[:], rhs=dts[:, 1, :]),
                           mmf(out=PS[7][:, 8:16], lhsT=ones[:], rhs=dts[:, 1, :])])
                    copy("dve", dts[:, 2, :], PS[7][:, 0:8])
                    kb.i("dve", "tensor_scalar", out=dts[:, 6, :], in0=PS[7][:, 0:8], scalar1=-1.0, scalar2=None, op0=ALU.mult)
                    kb.i("act", "activation", out=dts[:, 3, :], in_=PS[7][:, 0:8], func=AF.Exp)
                    TT_("dve", dts[:, 4, :], PS[7][:, 8:16], dts[:, 2, :], ALU.subtract)
                    kb.i("act", "activation", out=dts[:, 4, :], in_=dts[:, 4, :], func=AF.Exp)
                    TT_("dve", dts[:, 5, :], dts[:, 0, :], dts[:, 4, :], ALU.mult)
                    kb.i("act", "activation", out=dts[:, 7, :], in_=PS[7][:, 8:16], func=AF.Exp)
                    yield
                    kb.i("pool", "tensor_copy", out=dAb[:], in_=bc(dts[:, 1, :], [128, 8, 128]))
                    kb.pe([mmf(out=PS[6][:, g_ * 128:(g_ + 1) * 128], lhsT=xbcT[:, 4 + g_, :], rhs=xbcT[:, 6 + g_, :])
                           for g_ in range(2)])
                    TT_("dve", h8(xdt[:]), h8(x_tm[:]), bc(dts[:, 0, :], [128, 8, 64]), ALU.mult)
                    TT_("pool", h8(xdte[:]), h8(x_tm[:]), bc(dts[:, 5, :], [128, 8, 64]), ALU.mult)
                    yield
                    for h in range(8):
                        g_ = h // 4
                        bk = PS[6][:, 256 + (h % 2) * 128:384 + (h % 2) * 128]
                        kb.pe([("matmul", dict(out=bk, lhsT=dAb[:, h, :], rhs=triI[:], start=True, stop=False)),
                               ("matmul", dict(out=bk, lhsT=identf[:], rhs=NEGf[:], start=False, stop=True))],
                              _r=[b_c])
                        kb.i("act", "activation", out=dec[h % 2][:], in_=bk, func=AF.Exp, bias=dts[:, 6, h:h + 1])
                        TT_("dve", Lh[:, h, :], PS[6][:, g_ * 128:(g_ + 1) * 128], dec[h % 2][:], ALU.mult)
                        yield
                    kb.pe([mmf(out=PS[7][:, h * 64:(h + 1) * 64], lhsT=Lh[:, h, :], rhs=xdt[:, h * 64:(h + 1) * 64])
                           for h in range(8)])
                    kb.pe([mmf(out=PS[6][:, h * 64:(h + 1) * 64], lhsT=xbcT[:, 6 + h // 4, :], rhs=hbf[:, h * 64:(h + 1) * 64])
                           for h in range(8)])
                    TT_("dve", h8(ya[:]), h8(PS[6][:]), bc(dts[:, 3, :], [128, 8, 64]), ALU.mult)
                    TT_("dve", ya[:], PS[7][:], ya[:], ALU.add)
                    yield
                    TT_("pool", h8(yb[:]), h8(x_tm[:]), bc(sp8[:, 2, :], [128, 8, 64]), ALU.mult)
                    TT_("pool", ya[:], ya[:], yb[:], ALU.add)
                    TT_("dve", ya[:], ya[:], zs[:], ALU.mult)
                    yield
                    TT_("pool", yb[:], ya[:], ya[:], ALU.mult)
                    kb.i("dve", "tensor_reduce", out=st[:, 8:10], in_=yb[:].rearrange("p (g d) -> p g d", g=2), axis=AX.X,
                         op=ALU.add)
                    rstd(st[:, 8:10], st[:, 10:12], st[:, 12:14], 256, SSM_NORM_EPS)
                    yield
                    TT_("dve", mix_tm[:, 256:768].rearrange("p (g d) -> p g d", g=2),
                        ya[:].rearrange("p (g d) -> p g d", g=2), bc(st[:, 12:14], [128, 2, 256]), ALU.mult)
                    kb.pe([mmf(out=PS[7][:, h * 64:(h + 1) * 64], lhsT=B_tm[:, (h // 4) * 128:(h // 4 + 1) * 128],
                               rhs=xdte[:, h * 64:(h + 1) * 64]) for h in range(8)])
                    TT_("dve", h8(hst[:]), h8(hst[:]), bc(dts[:, 7, :], [128, 8, 64]), ALU.mult)
                    TT_("dve", hst[:], PS[7][:], hst[:], ALU.add)
                    copy("act", hbf[:], hst[:])
                def ssd_pull(n=1):
                    pass

                kb.rec = []
                fox_pull(10 ** 6)
                if upto >= 6:
                    fo_ = Et[0][:].bitcast(F32)
                    fo2_ = Et[1][:].bitcast(F32)
                    ov = PS[5][:].rearrange("p (h c) -> p h c", h=4)
                    kb.i("dve", "tensor_copy", out=st[:, 16:20], in_=ov[:, :, 64])
                    kb.i("dve", "reciprocal", out=st[:, 20:24], in_=st[:, 16:20])
                    TT0("dve", fo_, ov[:, :, 0:64], bc(st[:, 20:24], [128, 4, 64]), ALU.mult)
                    TT0("pool", fo2_, fo_, fo_, ALU.mult)
                    kb.i("dve", "tensor_reduce", out=st[:, 24:28], in_=fo2_, axis=AX.X, op=ALU.add)
                    rstd(st[:, 24:28], st[:, 28:32], st[:, 24:28], 64, NORM_EPS)
                    TT0("dve", mix_tm[:, 768:1024].rearrange("p (h d) -> p h d", h=4), fo_,
                        bc(st[:, 24:28], [128, 4, 64]), ALU.mult)
                if blk + 1 < NB and upto >= 7:
                    stage_A(hb[1 - s], True)
                    a_pre[0] = True
                Q2 = kb.rec
                kb.rec = []
                for _ in ssd_gen():
                    pass
                Q1 = kb.rec
                kb.rec = None
                N_RWKV = 210
                rw_done = [0]

                def il_hook():
                    rw_done[0] += 1
                    left = max(1, N_RWKV - rw_done[0])
                    for Q in (Q1, Q2):
                        for _ in range(-(-len(Q) // left)):
                            if Q:
                                kb.replay(Q.pop(0))
                kb.hook = il_hook
                if upto >= 4:
                    r_ = rk[:, 0:256]
                    k_r = rk[:, 256:512]
                    pb = lambda i_: pbc[:, i_, :]
                    h4 = lambda ap: ap.rearrange("p (h d) -> p h d", h=4)
                    TT_ = lambda e_, o_, a_, b_, op_: kb.i(e_, "tensor_tensor", out=o_, in0=a_, in1=b_, op=op_)
                    mmf = lambda **kw: ("matmul", dict(start=True, stop=True, **kw))
                    kb.pe([mmf(out=PS[1][:, 0:256], lhsT=lr_fm[0:32, :], rhs=lr2[0:32, :]),
                           mmf(out=PS[2][:, 0:256], lhsT=lr_fm[32:64, :], rhs=lr2[32:64, :]),
                           mmf(out=PS[3][:, 0:256], lhsT=lr_fm[64:128, :], rhs=lr2[64:128, :])])
                    TT_("dve", R["t1"][:], PS[1][:, 0:256], pb(0), ALU.add)
                    kb.i("act", "activation", out=R["sg"][:], in_=R["t1"][:], func=AF.Sigmoid)
                    TT_("dve", R["t2"][:], PS[2][:, 0:256], pb(1), ALU.add)
                    kb.i("act", "activation", out=R["a"][:], in_=R["t2"][:], func=AF.Sigmoid)
                    copy("act", R["g"][:], PS[3][:, 0:256])
                    TT_("pool", R["kk"][:], k_r, pb(2), ALU.mult)
                    TT_("dve", R["t1"][:], R["kk"][:], R["kk"][:], ALU.mult)
                    kb.i("dve", "tensor_reduce", out=st4[:, 0:4], in_=h4(R["t1"][:]), axis=AX.X, op=ALU.add)
                    kb.i("act", "activation", out=st4[:, 4:8], in_=st4[:, 0:4], func=AF.Ln, bias=1e-30)
                    kb.i("act", "activation", out=st4[:, 8:12], in_=st4[:, 4:8], func=AF.Exp, scale=-0.5)
                    TT_("dve", h4(R["kkn"][:]), h4(R["kk"][:]), bc(st4[:, 8:12], [128, 4, 64]), ALU.mult)
                    kb.i("dve", "scalar_tensor_tensor", out=R["t2"][:], in0=R["a"][:], scalar=-1.0, in1=pb(3),
                         op0=ALU.add, op1=ALU.mult)
                    kb.i("dve", "scalar_tensor_tensor", out=R["km"][:], in0=R["t2"][:], scalar=1.0, in1=k_r,
                         op0=ALU.add, op1=ALU.mult)
                    TT_("pool", R["b"][:], R["kkn"][:], R["a"][:], ALU.mult)
                    kb.pe([mmf(out=PS[1][:, 0:256], lhsT=triI[:], rhs=R["sg"][:]),
                           mmf(out=PS[1][:, 256:512], lhsT=ones[:], rhs=R["sg"][:]),
                           mmf(out=PS[2][:, 256:257], lhsT=R["sg"][:, 0:128], rhs=ones[:, 0:1]),
                           mmf(out=PS[2][:, 257:258], lhsT=R["sg"][:, 128:256], rhs=ones[:, 0:1])])
                    copy("dve", R["cs"][:], PS[1][:, 0:256])
                    kb.i("act", "activation", out=GC[:], in_=PS[2][:, 256:258], func=AF.Exp, scale=-C0)
                    kb.i("act", "activation", out=R["E"][:], in_=R["cs"][:], func=AF.Exp, scale=-C0)
                    kb.i("act", "activation", out=R["E2"][:], in_=R["cs"][:], func=AF.Exp, scale=C0)
                    TT_("dve", R["t1"][:], R["cs"][:], R["sg"][:], ALU.subtract)
                    TT_("dve", R["t2"][:], PS[1][:, 256:512], R["cs"][:], ALU.subtract)
                    kb.i("act", "activation", out=R["E3"][:], in_=R["t1"][:], func=AF.Exp, scale=-C0)
                    TT_("dve", R["rt"][:], r_, R["E"][:], ALU.mult)
                    TT_("dve", R["kt"][:], R["km"][:], R["E2"][:], ALU.mult)
                    TT_("pool", R["bt"][:], R["b"][:], R["E2"][:], ALU.mult)
                    kb.i("act", "activation", out=R["E"][:], in_=R["t2"][:], func=AF.Exp, scale=-C0)
                    kb.i("dve", "scalar_tensor_tensor", out=R["at"][:], in0=R["kkn"][:], scalar=-1.0, in1=R["E3"][:],
                         op0=ALU.mult, op1=ALU.mult)
                    TT_("dve", R["Kd"][:], R["km"][:], R["E"][:], ALU.mult)
                    TT_("pool", R["Bd"][:], R["b"][:], R["E"][:], ALU.mult)
                    TT_("dve", R["t1"][:], r_, R["km"][:], ALU.mult)
                    TT_("dve", R["t2"][:], R["t1"][:], pb(4), ALU.mult)
                    kb.i("dve", "tensor_reduce", out=st4[:, 12:16], in_=h4(R["t2"][:]), axis=AX.X, op=ALU.add)
                    for bank, names, FM in ((PS[3], ("at", "rt"), FMa), (PS[4], ("bt", "kt"), FMb)):
                        kb.pe([("transpose", dict(out=bank[:, (hp * 2 + x) * 128:(hp * 2 + x + 1) * 128],
                                                  in_=R[names[x]][:, hp * 128:(hp + 1) * 128], identity=identf[:]))
                               for hp in range(2) for x in range(2)], _r=[b_c])
                        copy(evac_eng(), FM[:].rearrange("p a b t -> p (a b t)"), bank[:])
                    for h in range(4):
                        hp, q_ = h // 2, h % 2
                        pr = slice(64 * q_, 64 * q_ + 64)
                        bA, bB = (PS[1], PS[2]) if h % 2 == 0 else (PS[3], PS[4])
                        ar = FMa[pr, hp, :, :].rearrange("p x t -> p (x t)")
                        bk_ = FMb[pr, hp, :, :].rearrange("p x t -> p (x t)")
                        kb.pe([mmf(out=bA[:, 0:256], lhsT=FMb[pr, hp, 0, :], rhs=ar),
                               mmf(out=bA[:, 256:384], lhsT=FMb[pr, hp, 1, :], rhs=FMa[pr, hp, 1, :]),
                               mmf(out=bB[:, 0:256], lhsT=FMa[pr, hp, 0, :], rhs=bk_)])
                        TT_("dve", Wv[h][:, 128:256], bA[:, 0:128], maskA[:, 0:128], ALU.mult)
                        TT_("dve", SA[h][:, 0:256], bA[:, 128:384], maskA[:, 128:384], ALU.mult)
                        TT_("dve", Wv[h][:, 0:128], bB[:, 0:128], maskB[:, 0:128], ALU.mult)
                        TT_("dve", SBt[h][:, 0:128], bB[:, 128:256], maskB[:, 128:256], ALU.mult)
                        kb.i("pool", "tensor_copy", out=Wv[h][:, 256:384], in_=identf[:], _r=[b_c])
                    for lv in range(1, 8):
                        last = (lv == 7)
                        for h in range(4):
                            bk = PS[1 + h]
                            if not last:
                                kb.pe([mmf(out=bk[:, 128:384], lhsT=Wv[h][:, 0:128], rhs=Wv[h][:, 128:384]),
                                       mmf(out=bk[:, 0:128], lhsT=Wv[h][:, 128:256], rhs=Wv[h][:, 0:128])])
                            else:
                                kb.pe([mmf(out=bk[:, 256:384], lhsT=Wv[h][:, 0:128], rhs=Wv[h][:, 256:384])])
                        for h in range(4):
                            bk = PS[1 + h]
                            if not last:
                                copy("act", Wv[h][:, 0:256], bk[:, 0:256])
                            TT_("dve", Wv[h][:, 256:384], bk[:, 256:384], Wv[h][:, 256:384], ALU.add)
                        fox_pull()
                        ssd_pull()
                    for h in range(4):
                        hp, q_ = h // 2, h % 2
                        pr = slice(64 * q_, 64 * q_ + 64)
                        bk = PS[1 + h]
                        kb.pe([mmf(out=bk[:, 0:128], lhsT=SBt[h][:, 0:128], rhs=Wv[h][:, 256:384]),
                               mmf(out=bk[pr, 128:256], lhsT=R["at"][:, h * 64:(h + 1) * 64], rhs=Wv[h][:, 256:384])])
                        copy("act", Mk2[h][:], bk[:, 0:128])
                        copy("dve", Wt[pr, hp, :], bk[pr, 128:256])
                        fox_pull()
                        ssd_pull()
                    def hd(h):
                        hp, q_ = h // 2, h % 2
                        return (hp, q_, slice(64 * q_, 64 * q_ + 64), slice(h * 64, (h + 1) * 64),
                                PS[1] if q_ == 0 else PS[2], PS[3] if q_ == 0 else PS[4],
                                slice(hp * 64, hp * 64 + 64), slice(256 + hp * 64, 256 + hp * 64 + 64))
                    for h in range(4):
                        hp, q_, pr, hs, bU, bY, us, so = hd(h)
                        kb.pe([("matmul", dict(out=bU[:, us], lhsT=Wt[pr, hp, :], rhs=ST[pr, hp, :], start=True, stop=False)),
                               ("matmul", dict(out=bU[:, us], lhsT=Mk2[h][:], rhs=v_tm[:, hs], start=False, stop=True))])
                    for q_ in range(2):
                        copy("act" if q_ == 0 else "dve",
                             U_tm[:].rearrange("p (hp q d) -> p hp q d", hp=2, q=2)[:, :, q_, :],
                             PS[1 + q_][:, 0:128].rearrange("p (hp d) -> p hp d", hp=2))
                    fox_pull()
                    ssd_pull()
                    for h in range(4):
                        hp, q_, pr, hs, bU, bY, us, so = hd(h)
                        kb.pe([("matmul", dict(out=bY[:, us], lhsT=FMa[pr, hp, 1, :], rhs=ST[pr, hp, :], start=True, stop=False)),
                               ("matmul", dict(out=bY[:, us], lhsT=SA[h][:, 128:256], rhs=v_tm[:, hs], start=False, stop=False)),
                               ("matmul", dict(out=bY[:, us], lhsT=SA[h][:, 0:128], rhs=U_tm[:, hs], start=False, stop=True))])
                        kb.pe([("matmul", dict(out=bU[pr, so], lhsT=R["Kd"][:, hs], rhs=v_tm[:, hs], start=True, stop=False)),
                               ("matmul", dict(out=bU[pr, so], lhsT=R["Bd"][:, hs], rhs=U_tm[:, hs], start=False, stop=True))])
                    for h in range(4):
                        hp, q_, pr, hs, bU, bY, us, so = hd(h)
                        kb.i("dve", "scalar_tensor_tensor", out=ST[pr, hp, :], in0=ST[pr, hp, :], scalar=GC[pr, hp:hp + 1],
                             in1=bU[pr, so], op0=ALU.mult, op1=ALU.add)
                    fox_pull()
                    ssd_pull()
                    fox_pull()
                    ssd_pull()
                    fox_pull()
                    ssd_pull()
                    for q_ in range(2):
                        copy("act", R["ysb"][:].rearrange("p (hp q d) -> p hp q d", hp=2, q=2)[:, :, q_, :],
                             PS[3 + q_][:, 0:128].rearrange("p (hp d) -> p hp d", hp=2))
                    kb.i("dve", "tensor_reduce", out=st4[:, 16:20], in_=h4(R["ysb"][:]), axis=AX.X, op=ALU.add)
                    kb.i("dve", "tensor_scalar", out=st4[:, 16:20], in0=st4[:, 16:20], scalar1=1.0 / 64, scalar2=None,
                         op0=ALU.mult)
                    TT_("dve", h4(R["yc"][:]), h4(R["ysb"][:]), bc(st4[:, 16:20], [128, 4, 64]), ALU.subtract)
                    TT_("pool", R["t1"][:], R["yc"][:], R["yc"][:], ALU.mult)
                    kb.i("dve", "tensor_reduce", out=st4[:, 20:24], in_=h4(R["t1"][:]), axis=AX.X, op=ALU.add)
                    rstd(st4[:, 20:24], st4[:, 24:28], st4[:, 28:32], 64, RWKV_GN_EPS)
                    TT_("dve", h4(R["yc"][:]), h4(R["yc"][:]), bc(st4[:, 28:32], [128, 4, 64]), ALU.mult)
                    TT_("pool", R["yc"][:], R["yc"][:], pb(5), ALU.mult)
                    TT_("pool", R["yc"][:], R["yc"][:], pb(6), ALU.add)
                    TT_("dve", h4(R["t2"][:]), h4(v_tm[:]), bc(st4[:, 12:16], [128, 4, 64]), ALU.mult)
                    TT_("dve", R["yc"][:], R["yc"][:], R["t2"][:], ALU.add)
                    TT_("dve", mix_tm[:, 0:256], R["yc"][:], R["g"][:], ALU.mult)
                kb.hook = None
                while Q1:
                    kb.replay(Q1.pop(0))
                while Q2:
                    kb.replay(Q2.pop(0))
                kb.rec = []
                psT = bf(PS[6][:]).rearrange("p (c t) -> p c t", c=8)
                kb.pe([("transpose", dict(out=psT[:, c, :], in_=mix_tm[:, c * 128:(c + 1) * 128], identity=identb[:]))
                       for c in range(KC)], _r=[b_c])
                kb.i("dve", "tensor_copy", out=mixT[:], in_=psT)
                for dn in range(2):
                    kb.pe([("matmul", dict(out=PS[6 + dn][:], lhsT=mixT[:, c, :],
                                           rhs=Wout[:, c, dn * 512:(dn + 1) * 512], start=(c == 0), stop=(c == KC - 1)))
                           for c in range(KC)])
                for dn in range(2):
                    kb.i("act", "activation", out=tt[:][:, dn * 512:(dn + 1) * 512], in_=PS[6 + dn][:], func=AF.Square,
                         accum_out=st[:, 4 + dn:5 + dn])
                kb.i("dve", "tensor_tensor", out=st[:, 4:5], in0=st[:, 4:5], in1=st[:, 5:6], op=ALU.add)
                rstd(st[:, 4:5], st[:, 5:6], st[:, 6:7], D, NORM_EPS)
                for dn in range(2):
                    kb.i("dve", "scalar_tensor_tensor", out=tt[:][:, dn * 512:(dn + 1) * 512], in0=PS[6 + dn][:],
                         scalar=st[:, 6:7], in1=gpost[:, dn * 512:(dn + 1) * 512], op0=ALU.mult, op1=ALU.mult)
                kb.i("pool", "tensor_tensor", out=hbs[:], in0=tt[:], in1=hbs[:], op=ALU.add)
                if dbg is not None:
                    kb.d("sp", dbg["mix"][row0:row0 + 128, :], mix_tm[:])
                kb.d("sp", h_out[row0:row0 + 128, :], hbs[:], _w=[hout_bufs[seq * NB + blk]])
                QF.extend(kb.rec)
                kb.rec = None
        while QF:
            kb.replay(QF.pop(0))
        kb.barrier()


PNAMES = ["norm_mix_pre", "norm_mix_post", "w_in", "rwkv_mu", "rwkv_w0", "rwkv_w2", "rwkv_a0", "rwkv_a2", "rwkv_g2",
          "rwkv_k_k", "rwkv_k_a", "rwkv_r_k", "rwkv_ln_w", "rwkv_ln_b", "ssm_conv_w", "ssm_conv_b", "ssm_dt_bias",
          "ssm_A_log", "ssm_D", "ssm_norm_w", "fox_f_bias", "fox_norm_w", "w_out", "norm_mlp_pre", "norm_mlp_post",
          "w_mlp_up", "w_mlp_down"]
N_CORES = 8
_CACHE = {}


def build_program(shapes, nseq, T, depth, phases=None):
    nc = bass.Bass("TRN2", target_bir_lowering=False, dynamic_dma_scratch_size=1024)
    kb = KB(nc)
    ntok = nseq * T
    x = nc.dram_tensor("x", [ntok, D], F32, kind="ExternalInput").ap()
    out = nc.dram_tensor("out", [ntok, D], F32, kind="ExternalOutput").ap()
    P = {n: nc.dram_tensor(n, list(shapes[n]), F32, kind="ExternalInput").ap() for n in PNAMES}
    hA = nc.dram_tensor("hA", [ntok, D], F32, kind="Internal").ap()
    hB = nc.dram_tensor("hB", [ntok, D], F32, kind="Internal").ap()
    nchunk = ntok // 128
    cst = make_consts(kb, nc)
    seqp = []
    for l in range(depth):
        seqp += [("mix", l), ("mlp", l)]
    if phases is not None:
        seqp = seqp[:phases]
    cur, cur_b = x, [Buf(f"x{i}") for i in range(nchunk)]
    scratch = [hA, hB]
    for pi, (kind, l) in enumerate(seqp):
        last = (pi == len(seqp) - 1)
        dst = out if last else scratch[pi % 2]
        dst_b = [Buf(f"h{pi}_{i}") for i in range(nchunk)]
        if kind == "mix":
            mix_phase(kb, nc, cst, cur, dst, cur_b, dst_b, P, l, nseq, T)
        else:
            mlp_phase(kb, nc, cst, cur, dst, cur_b, dst_b, P["w_mlp_up"][l], P["w_mlp_down"][l],
                      P["norm_mlp_pre"][l], P["norm_mlp_post"][l], ntok)
            kb.barrier()
        cur, cur_b = dst, dst_b
    kb.wait_all("sp", cur_b)
    return nc, kb


def kernel(**inputs):
    from concourse.bass_utils import run_bass_kernel_spmd
    x = np.ascontiguousarray(np.asarray(inputs["x"], dtype=np.float32))
    Bsz, T, Dm = x.shape
    depth = inputs["w_in"].shape[0]
    nseq = Bsz // N_CORES
    params = {n: np.ascontiguousarray(np.asarray(inputs[n], dtype=np.float32)) for n in PNAMES}
    key = (Bsz, T, depth)
    if key not in _CACHE:
        _CACHE[key] = build_program({n: params[n].shape for n in PNAMES}, nseq, T, depth)
    nc, kb = _CACHE[key]
    in_maps = []
    for c in range(N_CORES):
        m = dict(params)
        m["x"] = x[c * nseq:(c + 1) * nseq].reshape(nseq * T, Dm)
        in_maps.append(m)
    res = run_bass_kernel_spmd(nc, in_maps, core_ids=list(range(N_CORES)))
    outs = [np.asarray(res.results[c]["out"]).reshape(nseq, T, Dm) for c in range(N_CORES)]
    return np.concatenate(outs, axis=0).astype(np.float32)
```

```python
import numpy as np
from contextlib import ExitStack
import concourse.bass as bass
import concourse.mybir as mybir

F32 = mybir.dt.float32
BF16 = mybir.dt.bfloat16
AF = mybir.ActivationFunctionType
ALU = mybir.AluOpType
AX = mybir.AxisListType

EPOCH = 30000


class Buf:
    __slots__ = ("name", "lw", "rd", "dsem", "dcnt")

    def __init__(self, name=""):
        self.name = name
        self.lw = None
        self.rd = {}
        self.dsem = None
        self.dcnt = 0


class KB:
    def __init__(self, nc):
        self.nc = nc
        self.E = {"pe": nc.tensor, "act": nc.scalar, "dve": nc.vector, "pool": nc.gpsimd, "sp": nc.sync}
        self.esems = {e: [] for e in self.E}
        self.ecnt = {e: 0 for e in self.E}
        self.seen = {e: {} for e in self.E}
        self.nsem = 0
        self.n_ins = 0
        self.n_wait = 0
        self.reg = {}
        self.dbufs = []
        self.limit = 10 ** 9
        self.uid = 0
        self.rec = None
        self.hook = None
        self.in_hook = False

    def newsem(self, name):
        self.nsem += 1
        return self.nc.alloc_semaphore(f"s{self.nsem}_{name}")

    def _wait(self, eng, tok):
        sem, val = tok
        k = id(sem)
        if self.seen[eng].get(k, 0) >= val:
            return
        self.seen[eng][k] = val
        self.E[eng].wait_ge(sem, val)
        self.n_wait += 1

    def _deps(self, eng, reads, writes):
        toks = []
        for b in reads:
            if b.lw is not None:
                toks.append(b.lw)
        for b in writes:
            if b.lw is not None:
                toks.append(b.lw)
            toks.extend(b.rd.values())
        own = self.esems[eng]
        for t in toks:
            if eng == "pe" and any(t[0] is s for s in own):
                continue
            self._wait(eng, t)

    def _commit(self, tok, reads, writes):
        k = id(tok[0])
        for b in reads:
            b.rd[k] = tok
        for b in writes:
            b.lw = tok
            b.rd = {}

    def _etok(self, eng, ins):
        c = self.ecnt[eng]
        ep, pos = divmod(c, EPOCH)
        if ep >= len(self.esems[eng]):
            self.esems[eng].append(self.newsem(f"{eng}{ep}"))
        sem = self.esems[eng][ep]
        ins.then_inc(sem, 1)
        self.ecnt[eng] = c + 1
        return (sem, pos + 1)

    def op(self, eng, fn, reads=(), writes=()):
        if self.rec is not None:
            self.rec.append((eng, fn, list(reads), list(writes)))
            return None
        if self.n_ins >= self.limit:
            return None
        self._deps(eng, reads, writes)
        ins = fn(self.E[eng])
        tok = self._etok(eng, ins)
        self._commit(tok, reads, writes)
        self.n_ins += 1
        if self.hook is not None and not self.in_hook:
            self.in_hook = True
            self.hook()
            self.in_hook = False
        return tok

    def replay(self, item):
        sv, self.in_hook = self.in_hook, True
        if item[0] == "__dma__":
            q, out, in_, reads, writes, sembuf, kw = item[1]
            self.dma(q, out, in_, reads=reads, writes=writes, sembuf=sembuf, **kw)
        else:
            self.op(*item)
        self.in_hook = sv

    def dma(self, q, out, in_, reads=(), writes=(), sembuf=None, **kw):
        if self.rec is not None:
            self.rec.append(("__dma__", (q, out, in_, list(reads), list(writes), sembuf, kw)))
            return None
        if self.n_ins >= self.limit:
            return None
        self._deps(q, reads, writes)
        sb = sembuf if sembuf is not None else (writes[0] if writes else reads[0])
        if sb.dsem is None:
            sb.dsem = self.newsem("d" + sb.name)
            self.dbufs.append(sb)
        sb.dcnt += 16
        self.E[q].dma_start(out=out, in_=in_, **kw).then_inc(sb.dsem, 16)
        tok = (sb.dsem, sb.dcnt)
        self._commit(tok, reads, writes)
        self.n_ins += 1
        return tok

    def T(self, es, name, shape, dt, psum=False):
        name = f"{name}_{self.uid}"
        t = es.enter_context((self.nc.psum_tensor if psum else self.nc.sbuf_tensor)(name, shape, dt))
        self.reg[t.name] = Buf(name)
        return t

    def bufs_of(self, kwargs, extra_r=(), extra_w=()):
        r, w = list(extra_r), list(extra_w)
        for k, v in kwargs.items():
            if type(v).__name__ == "AP":
                b = self.reg.get(v.tensor.name)
                if b is None:
                    continue
                if v.tensor.name.startswith("PS"):
                    r.append(b)
                    w.append(b)
                else:
                    (w if k in ("out", "accum_out") else r).append(b)
        return r, w

    def i(self, eng, meth, _r=(), _w=(), **kwargs):
        r, w = self.bufs_of(kwargs, _r, _w)
        return self.op(eng, lambda e: getattr(e, meth)(**kwargs), reads=r, writes=w)

    def pe(self, calls, _r=(), _w=()):
        r, w = list(_r), list(_w)
        for m, kw in calls:
            r2, w2 = self.bufs_of(kw)
            r += r2
            w += w2

        def fn(e):
            for m, kw in calls:
                ins = getattr(e, m)(**kw)
            return ins
        return self.op("pe", fn, reads=r, writes=w)

    def d(self, q, out, in_, _r=(), _w=(), **kw):
        r, w = self.bufs_of({"out": out, "in_": in_}, _r, _w)
        sb = None
        for b in w + r:
            sb = b
            break
        ob = self.reg.get(out.tensor.name)
        ib = self.reg.get(in_.tensor.name)
        sb = ob if ob is not None else (ib if ib is not None else sb)
        return self.dma(q, out, in_, reads=r, writes=w, sembuf=sb, **kw)

    def barrier(self):
        toks = []
        for e in self.E:
            c = self.ecnt[e]
            if c:
                ep, pos = divmod(c - 1, EPOCH)
                toks.append((self.esems[e][ep], pos + 1))
        for b in self.dbufs:
            toks.append((b.dsem, b.dcnt))
        for e in self.E:
            for t in toks:
                if e == "pe" and any(t[0] is s for s in self.esems[e]):
                    continue
                self._wait(e, t)

    def wait_all(self, eng, bufs):
        for b in bufs:
            if b.lw is not None:
                self._wait(eng, b.lw)
            for t in b.rd.values():
                self._wait(eng, t)


D = 1024
DFF = 4096
NORM_EPS = 1e-6


def make_consts(kb, nc):
    c = {}
    c["ident_f"] = nc.alloc_sbuf_tensor("ident_f", [128, 128], F32)
    c["ident_b"] = nc.alloc_sbuf_tensor("ident_b", [128, 128], BF16)
    b = Buf("consts")
    c["buf"] = b
    idf, idb = c["ident_f"], c["ident_b"]
    kb.op("pool", lambda e: e.memset(idf[:], 1.0), writes=[b])
    kb.op("pool", lambda e: e.affine_select(out=idf[:], in_=idf[:], pattern=[[-1, 128]],
                                            compare_op=ALU.is_equal, fill=0.0, base=0, channel_multiplier=1),
          reads=[b], writes=[b])
    kb.op("pool", lambda e: e.tensor_copy(out=idb[:], in_=idf[:]), reads=[b], writes=[b])
    return c


def rstd_from_ss(kb, ss_ap, tmp_ap, out_ap, n, eps, bufs_r, bufs_w):
    kb.op("act", lambda e: e.activation(out=tmp_ap, in_=ss_ap, func=AF.Ln, scale=1.0 / n, bias=eps),
          reads=bufs_r, writes=bufs_w)
    kb.op("act", lambda e: e.activation(out=out_ap, in_=tmp_ap, func=AF.Exp, scale=-0.5),
          reads=bufs_w, writes=bufs_w)


def mlp_phase(kb, nc, cst, h_in, h_out, hin_bufs, hout_bufs, w_up, w_dn, g_pre, g_post, ntok, TT=256, dbg=None):
    kb.uid += 1
    J = TT // 128
    NT = ntok // TT
    KC = D // 128
    FC = DFF // 128
    with ExitStack() as es:
        wup = es.enter_context(nc.sbuf_tensor(f"wup_{kb.uid}", [128, KC, DFF], BF16))
        wdn = es.enter_context(nc.sbuf_tensor(f"wdn_{kb.uid}", [128, FC, D], BF16))
        gpre = es.enter_context(nc.sbuf_tensor(f"gpre_{kb.uid}", [128, KC], F32))
        gpost = es.enter_context(nc.sbuf_tensor(f"gpost_{kb.uid}", [128, D], F32))
        b_w = Buf("mlpw")
        b_g = Buf("mlpg")
        kb.dma("sp", gpre[:], g_pre.rearrange("(c p) -> p c", p=128), writes=[b_g], allow_slow_non_contiguous=True)
        kb.dma("sp", gpost[:], g_post.rearrange("(o n) -> o n", o=1).to_broadcast([128, D]), writes=[b_g])
        hb0 = es.enter_context(nc.sbuf_tensor(f"hb0_{kb.uid}", [128, J, D], F32))
        hb1 = es.enter_context(nc.sbuf_tensor(f"hb1_{kb.uid}", [128, J, D], F32))
        u = es.enter_context(nc.sbuf_tensor(f"u_{kb.uid}", [128, J, D], BF16))
        uT0 = es.enter_context(nc.sbuf_tensor(f"uT0_{kb.uid}", [128, KC, TT], BF16))
        uT1 = es.enter_context(nc.sbuf_tensor(f"uT1_{kb.uid}", [128, KC, TT], BF16))
        hT = es.enter_context(nc.sbuf_tensor(f"hT_{kb.uid}", [128, FC, TT], BF16))
        r0 = es.enter_context(nc.sbuf_tensor(f"r0_{kb.uid}", [128, TT], F32))
        r1 = es.enter_context(nc.sbuf_tensor(f"r1_{kb.uid}", [128, TT], F32))
        tt = es.enter_context(nc.sbuf_tensor(f"tt_{kb.uid}", [128, D], F32))
        st = es.enter_context(nc.sbuf_tensor(f"st_{kb.uid}", [128, 16], F32))
        psT0 = es.enter_context(nc.psum_tensor(f"psT0_{kb.uid}", [128, KC, 128], BF16))
        psT1 = es.enter_context(nc.psum_tensor(f"psT1_{kb.uid}", [128, KC, 128], BF16))
        psH0 = es.enter_context(nc.psum_tensor(f"psH0_{kb.uid}", [128, 512], F32))
        psH1 = es.enter_context(nc.psum_tensor(f"psH1_{kb.uid}", [128, 512], F32))
        psO0 = es.enter_context(nc.psum_tensor(f"psO0_{kb.uid}", [128, D], F32))
        psO1 = es.enter_context(nc.psum_tensor(f"psO1_{kb.uid}", [128, D], F32))
        hb = [hb0, hb1]
        uT = [uT0, uT1]
        rr = [r0, r1]
        psT = [psT0, psT1]
        psH = [psH0, psH1]
        psO = [psO0, psO1]
        b_hb = [[Buf(f"hb{s}{j}") for j in range(J)] for s in range(2)]
        b_u = [Buf(f"u{j}") for j in range(J)]
        b_uT = [Buf("uT0"), Buf("uT1")]
        b_hT = [Buf(f"hT{f}") for f in range(FC)]
        b_r = [Buf("r0"), Buf("r1")]
        b_tt = Buf("tt")
        b_st = [Buf("st0"), Buf("st1")]
        b_psT = [Buf("psT0"), Buf("psT1")]
        b_psH = [Buf("psH0"), Buf("psH1")]
        b_psO = [Buf("psO0"), Buf("psO1")]
        identb = cst["ident_b"]
        b_c = cst["buf"]

        if True:
            assert J == 2
            stg = [hb0[:].rearrange("p j d -> p (j d)"), hb1[:].rearrange("p j d -> p (j d)")]
            b_s = [b_hb[0], b_hb[1]]
            n = 0
            for kc in range(KC):
                for hf in range(2):
                    s = n % 2
                    kb.dma("sp", stg[s], w_up[kc * 128:(kc + 1) * 128, hf * 2048:(hf + 1) * 2048], writes=b_s[s])
                    dst = wup[:, kc, hf * 2048:(hf + 1) * 2048]
                    if n % 2 == 0:
                        kb.op("act", lambda e: e.activation(out=dst, in_=stg[s], func=AF.Copy,
                                                            scale=gpre[:, kc:kc + 1]),
                              reads=b_s[s] + [b_g], writes=[b_w])
                    else:
                        kb.op("dve", lambda e: e.tensor_scalar(out=dst, in0=stg[s], scalar1=gpre[:, kc:kc + 1],
                                                               scalar2=None, op0=ALU.mult),
                              reads=b_s[s] + [b_g], writes=[b_w])
                    n += 1
            for fc in range(0, FC, 2):
                s = n % 2
                kb.dma("sp", stg[s].rearrange("p (c n) -> p c n", c=2),
                       w_dn[fc * 128:(fc + 2) * 128, :].rearrange("(c p) n -> p c n", p=128), writes=b_s[s])
                dst = wdn[:, fc:fc + 2, :]
                src = stg[s].rearrange("p (c n) -> p c n", c=2)
                eng = ["act", "dve", "pool"][(fc // 2) % 3]
                if eng == "act":
                    kb.op("act", lambda e: e.copy(out=dst, in_=src), reads=b_s[s], writes=[b_w])
                else:
                    kb.op(eng, lambda e: e.tensor_copy(out=dst, in_=src), reads=b_s[s], writes=[b_w])
                n += 1
        def load(i):
            s = i % 2
            for j in range(J):
                r0_ = i * TT + j * 128
                kb.dma("sp", hb[s][:, j, :], h_in[r0_:r0_ + 128, :],
                       reads=[hin_bufs[r0_ // 128]], writes=[b_hb[s][j]], sembuf=b_hb[s][j])

        nps = [0]

        def prenorm(i):
            s = i % 2
            stt = st[:, (s * 8):(s * 8 + 8)]
            for j in range(J):
                kb.op("act", lambda e: e.activation(out=u[:, j, :], in_=hb[s][:, j, :], func=AF.Square,
                                                    accum_out=stt[:, j:j + 1]),
                      reads=[b_hb[s][j]], writes=[b_u[j], b_st[s]])
            rstd_from_ss(kb, stt[:, 0:J], stt[:, 2:2 + J], stt[:, 4:4 + J], D, NORM_EPS, [b_st[s]], [b_st[s]])
            for j in range(J):
                kb.op("dve", lambda e: e.tensor_scalar(out=u[:, j, :], in0=hb[s][:, j, :],
                                                       scalar1=stt[:, 4 + j:5 + j], scalar2=None, op0=ALU.mult),
                      reads=[b_hb[s][j], b_st[s]], writes=[b_u[j]])
                p = nps[0] % 2
                nps[0] += 1

                def tr(e):
                    for kc in range(KC):
                        ins = e.transpose(out=psT[p][:, kc, :], in_=u[:, j, kc * 128:(kc + 1) * 128],
                                          identity=identb[:])
                    return ins
                kb.op("pe", tr, reads=[b_u[j], b_c], writes=[b_psT[p]])
                kb.op("dve", lambda e: e.tensor_copy(out=uT[s][:, :, j * 128:(j + 1) * 128], in_=psT[p][:]),
                      reads=[b_psT[p]], writes=[b_uT[s]])

        load(0)
        if NT > 1:
            load(1)
        prenorm(0)
        for i in range(NT):
            s = i % 2
            stt = st[:, (s * 8):(s * 8 + 8)]
            for f in range(FC):
                p = f % 2

                def up(e):
                    for kc in range(KC):
                        ins = e.matmul(psH[p][:, 0:TT], lhsT=wup[:, kc, f * 128:(f + 1) * 128], rhs=uT[s][:, kc, :],
                                       start=(kc == 0), stop=(kc == KC - 1))
                    return ins
                kb.op("pe", up, reads=[b_w, b_uT[s]], writes=[b_psH[p]])
                kb.op("act", lambda e: e.activation(out=rr[p][:], in_=psH[p][:, 0:TT], func=AF.Relu),
                      reads=[b_psH[p]], writes=[b_r[p]])
                kb.op("dve" if p == 0 else "pool",
                      lambda e: e.tensor_tensor(out=hT[:, f, :], in0=rr[p][:], in1=rr[p][:], op=ALU.mult),
                      reads=[b_r[p]], writes=[b_hT[f]])
            if dbg is not None and i == 0:
                kb.dma("sp", dbg["u"], u[:], reads=b_u, sembuf=b_u[0])
                kb.dma("sp", dbg["uT"], uT[s][:], reads=[b_uT[s]], sembuf=b_uT[s])
                kb.dma("sp", dbg["hT"], hT[:], reads=b_hT, sembuf=b_hT[0])
                kb.dma("sp", dbg["st"], st[:], reads=b_st, sembuf=b_st[0])
                kb.dma("sp", dbg["wup"], wup[:, 0, :], reads=[b_w], sembuf=b_w)
                kb.dma("sp", dbg["idb"], identb[:], reads=[b_c], sembuf=b_c)
                kb.dma("sp", dbg["wdn"], wdn[:], reads=[b_w], sembuf=b_w)
            if i + 1 < NT:
                prenorm(i + 1)
            for j in range(J):
                p = j % 2
                for dn in range(2):
                    def down(e):
                        for f in range(FC):
                            ins = e.matmul(psO[p][:, dn * 512:(dn + 1) * 512], lhsT=hT[:, f, j * 128:(j + 1) * 128],
                                           rhs=wdn[:, f, dn * 512:(dn + 1) * 512],
                                           start=(f == 0), stop=(f == FC - 1))
                        return ins
                    kb.op("pe", down, reads=[b_w] + b_hT, writes=[b_psO[p]])
                kb.op("act", lambda e: e.activation(out=u[:, j, :], in_=psO[p][:], func=AF.Square,
                                                    accum_out=stt[:, 6:7]),
                      reads=[b_psO[p]], writes=[b_u[j], b_st[s]])
                rstd_from_ss(kb, stt[:, 6:7], stt[:, 7:8], stt[:, 6:7], D, NORM_EPS, [b_st[s]], [b_st[s]])
                kb.op("dve", lambda e: e.scalar_tensor_tensor(out=tt[:], in0=psO[p][:], scalar=stt[:, 6:7],
                                                              in1=gpost[:], op0=ALU.mult, op1=ALU.mult),
                      reads=[b_psO[p], b_st[s], b_g], writes=[b_tt])
                kb.op("pool", lambda e: e.tensor_tensor(out=hb[s][:, j, :], in0=tt[:], in1=hb[s][:, j, :], op=ALU.add),
                      reads=[b_tt, b_hb[s][j]], writes=[b_hb[s][j]])
                r0_ = i * TT + j * 128
                kb.dma("sp", h_out[r0_:r0_ + 128, :], hb[s][:, j, :],
                       reads=[b_hb[s][j]], writes=[hout_bufs[r0_ // 128]], sembuf=b_hb[s][j])
            if i + 2 < NT:
                load(i + 2)


C0 = 0.6065306597126334
RWKV_GN_EPS = 64e-5
SSM_NORM_EPS = 1e-5
NCOL = 3212
import os as _os0
IDT = F32
NCOLP = 3232


def mix_phase(kb, nc, cst, h_in, h_out, hin_bufs, hout_bufs, P, l, nseq, T, dbg=None, upto=99):
    NB = T // 128
    KC = 8
    kb.uid += 1
    with ExitStack() as es:
        def sb(name, shape, dt=F32):
            return kb.T(es, name, shape, dt)

        Win = sb("Win", [128, KC, NCOLP], BF16)
        WinB = sb("WinB", [128, KC, 896], BF16)
        Wout = sb("Wout", [128, KC, D], BF16)
        gpre = sb("gpre", [128, KC])
        gpost = sb("gpostm", [128, D])
        rs = sb("rs", [128, KC])
        pbc = sb("pbc", [128, 7, 256])
        lr2 = sb("lr2", [128, 256])
        cw = sb("cw", [128, 8, 4])
        cb = sb("cb", [128, 8])
        sp8 = sb("sp8", [128, 4, 8])
        fb4 = sb("fb4", [128, 4])
        kb.d("sp", gpre[:], P["norm_mix_pre"][l].rearrange("(c p) -> p c", p=128), allow_slow_non_contiguous=True)
        kb.d("sp", gpost[:], P["norm_mix_post"][l:l + 1, :].to_broadcast([128, D]))
        for n_, nm in enumerate(["rwkv_w0", "rwkv_a0", "rwkv_k_k", "rwkv_k_a"]):
            kb.d("sp", pbc[:, n_, :], P[nm][l:l + 1, :].to_broadcast([128, 256]))
        kb.d("sp", pbc[:, 4, :], P["rwkv_r_k"][l:l + 1].rearrange("o h n -> o (h n)").to_broadcast([128, 256]))
        kb.d("sp", pbc[:, 5, :], P["rwkv_ln_w"][l:l + 1, :].to_broadcast([128, 256]))
        kb.d("sp", pbc[:, 6, :], P["rwkv_ln_b"][l:l + 1, :].to_broadcast([128, 256]))
        kb.d("sp", lr2[0:32, :], P["rwkv_w2"][l])
        kb.d("sp", lr2[32:64, :], P["rwkv_a2"][l])
        kb.d("sp", lr2[64:128, :], P["rwkv_g2"][l])
        for k_ in range(4):
            kb.d("sp", cw[:, :, k_], P["ssm_conv_w"][l, k_].rearrange("(c p) -> p c", p=128),
                 allow_slow_non_contiguous=True)
        kb.d("sp", cb[:], P["ssm_conv_b"][l].rearrange("(c p) -> p c", p=128), allow_slow_non_contiguous=True)
        kb.d("sp", sp8[:, 0, :], P["ssm_dt_bias"][l:l + 1, :].to_broadcast([128, 8]))
        kb.d("sp", sp8[:, 1, :], P["ssm_A_log"][l:l + 1, :].to_broadcast([128, 8]))
        kb.d("sp", sp8[:, 2, :], P["ssm_D"][l:l + 1, :].to_broadcast([128, 8]))
        kb.d("sp", fb4[:], P["fox_f_bias"][l:l + 1, :].to_broadcast([128, 4]))
        kb.i("pool", "memset", ap=rs[:], constant=1.0, _w=[kb.reg[rs.name]])
        kb.d("sp", rs[:, 2:6], P["ssm_norm_w"][l].rearrange("(c p) -> p c", p=128), allow_slow_non_contiguous=True)
        kb.d("sp", rs[:, 6:8], P["fox_norm_w"][l].rearrange("(c p) -> p c", p=128), allow_slow_non_contiguous=True)
        kb.i("act", "activation", out=sp8[:, 1, :], in_=sp8[:, 1, :], func=AF.Exp)
        kb.i("dve", "tensor_scalar", out=sp8[:, 1, :], in0=sp8[:, 1, :], scalar1=-1.0, scalar2=None, op0=ALU.mult)

        with ExitStack() as es2:
            stg = [kb.T(es2, "stgA", [128, NCOL], F32), kb.T(es2, "stgB", [128, NCOL], F32)]
            mu_bc = kb.T(es2, "mu_bc", [128, 896], F32)
            omu_bc = kb.T(es2, "omu_bc", [128, 896], F32)
            kb.d("sp", mu_bc[:], P["rwkv_mu"][l:l + 1, :].to_broadcast([128, 896]))
            kb.i("dve", "tensor_scalar", out=omu_bc[:], in0=mu_bc[:], scalar1=-1.0, scalar2=1.0, op0=ALU.mult, op1=ALU.add)
            n = 0
            for kc in range(KC):
                s_ = stg[n % 2]
                n += 1
                kb.d("sp", s_[:], P["w_in"][l, kc * 128:(kc + 1) * 128, :])
                g_ = gpre[:, kc:kc + 1]
                kb.i("dve", "scalar_tensor_tensor", out=Win[:, kc, 0:896], in0=s_[:, 0:896], scalar=g_,
                     in1=omu_bc[:], op0=ALU.mult, op1=ALU.mult)
                kb.i("dve", "scalar_tensor_tensor", out=WinB[:, kc, 0:896], in0=s_[:, 0:896], scalar=g_,
                     in1=mu_bc[:], op0=ALU.mult, op1=ALU.mult)
                kb.i("act", "activation", out=Win[:, kc, 896:2432], in_=s_[:, 896:2432], func=AF.Copy, scale=g_)
                kb.i("pool", "tensor_scalar", out=Win[:, kc, 2432:3204], in0=s_[:, 2440:3212], scalar1=g_,
                     scalar2=None, op0=ALU.mult)
                kb.i("pool", "tensor_scalar", out=Win[:, kc, 3204:3212], in0=s_[:, 2432:2440], scalar1=g_,
                     scalar2=None, op0=ALU.mult)
            for c in range(KC):
                s_ = stg[n % 2]
                n += 1
                kb.d("sp", s_[:, 0:D], P["w_out"][l, c * 128:(c + 1) * 128, :])
                if c % 2:
                    kb.i("act", "activation", out=Wout[:, c, :], in_=s_[:, 0:D], func=AF.Copy, scale=rs[:, c:c + 1])
                else:
                    kb.i("dve", "tensor_scalar", out=Wout[:, c, :], in0=s_[:, 0:D], scalar1=rs[:, c:c + 1],
                         scalar2=None, op0=ALU.mult)
            kb.barrier()
        kb.barrier()

        identf, identb = cst["ident_f"], cst["ident_b"]
        b_c = cst["buf"]
        triI = sb("triI", [128, 128])
        maskA = sb("maskA", [128, 384])
        maskB = sb("maskB", [128, 256])
        ones = sb("ones", [128, 128])
        halfm = sb("halfm", [128, 128])
        NEGf = sb("NEGf", [128, 128])
        NEGb = sb("NEGb", [128, 128], BF16)
        kb.i("pool", "memset", ap=ones[:], constant=1.0, _w=[kb.reg[ones.name]])
        kb.i("pool", "memset", ap=halfm[:], constant=0.0, _w=[kb.reg[halfm.name]])
        kb.i("pool", "memset", ap=halfm[0:64, :], constant=1.0, _w=[kb.reg[halfm.name]])

        def asel(out_ap, in_ap, pat, cm, op, fill):
            kb.i("pool", "affine_select", out=out_ap, in_=in_ap, pattern=pat, compare_op=op, fill=fill, base=0,
                 channel_multiplier=cm)
        asel(triI[:], ones[:], [[1, 128]], -1, ALU.is_ge, 0.0)
        asel(maskA[:, 0:128], ones[:], [[1, 128]], -1, ALU.is_gt, 0.0)
        kb.i("pool", "tensor_copy", out=maskA[:, 128:256], in_=triI[:])
        kb.i("pool", "tensor_copy", out=maskA[:, 256:384], in_=triI[:])
        asel(maskB[:, 0:128], ones[:], [[-1, 128]], 1, ALU.is_gt, 0.0)
        kb.i("pool", "tensor_copy", out=maskB[:, 128:256], in_=maskB[:, 0:128])
        kb.i("pool", "tensor_scalar", out=NEGf[:], in0=maskB[:, 0:128], scalar1=-1.0e4, scalar2=None, op0=ALU.mult)
        kb.i("pool", "tensor_scalar", out=NEGb[:], in0=maskB[:, 0:128], scalar1=-30000.0, scalar2=None, op0=ALU.mult)

        hb = [sb("mhb0", [128, D]), sb("mhb1", [128, D])]
        u = sb("mu_", [128, D], BF16)
        uT = sb("muT0", [128, KC, 128], BF16)
        uTp = sb("muTp", [128, KC, 128], BF16)
        st = sb("mst", [128, 32])
        rk = sb("rk", [128, 512])
        v_tm = sb("v_tm", [128, 256])
        zs = sb("zs", [128, 512])
        lr_fm = sb("lr_fm", [128, 128])
        xraw = sb("xraw0", [128, 8, 132])
        xcar = sb("xcar", [128, 8, 4])
        xacc = sb("xacc", [128, 8, 128])
        xbcT = sb("xbcT", [128, 8, 128], BF16)
        x_tm = sb("x_tm", [128, 512], BF16)
        B_tm = sb("B_tm", [128, 256], BF16)
        qT = sb("qT", [128, 2, 128], BF16)
        KT = [sb(f"KT{j}", [128, 2, 128], BF16) for j in range(NB)]
        VA = [sb(f"VA{j}", [128, 4, 66], BF16) for j in range(NB)]
        tok8 = sb("tok8", [128, 16])
        R = {nm: sb("rw_" + nm, [128, 256]) for nm in
             ["sg", "a", "g", "kk", "kkn", "km", "b", "cs", "t1", "t2", "E", "E2", "E3", "rt", "kt", "bt", "at", "Kd", "Bd",
              ]}
        R["ysb"] = R["kt"]
        R["yc"] = R["bt"]
        st4 = sb("st4", [128, 32])
        FMa = sb("FMa", [128, 2, 2, 128])
        FMb = sb("FMb", [128, 2, 2, 128])
        SA = [sb(f"SA{h}", [128, 256]) for h in range(4)]
        SBt = [sb(f"SBt{h}", [128, 128], IDT) for h in range(4)]
        Wv = [sb(f"Wv{h}", [128, 384], IDT) for h in range(4)]
        Mk2 = [sb(f"Mk2{h}", [128, 128]) for h in range(4)]
        Wt = sb("Wt", [128, 2, 128])
        U_tm = sb("U_tm", [128, 256])
        ST = sb("STs", [128, 2, 64])
        GC = sb("GC", [128, 2])
        dts = sb("dts", [128, 8, 8])
        dAb = sb("dAb", [128, 8, 128])
        dec = [sb("dec0", [128, 128]), sb("dec1", [128, 128])]
        Lh = sb("Lh", [128, 8, 128], BF16)
        xdt = sb("xdt", [128, 512], BF16)
        xdte = sb("xdte", [128, 512], BF16)
        hst = sb("hst", [128, 512])
        hbf = sb("hbf", [128, 512], BF16)
        Et = [sb("Et0", [128, 4, 128], BF16), sb("Et1", [128, 4, 128], BF16)]
        ckg = sb("ckg", [128, NB, 4])
        ball = sb("ball", [128, NB, 4])
        prefix = sb("prefix", [128, 4])
        refn = sb("refn", [128, 4])
        mix_tm = sb("mix_tm", [128, D], BF16)
        mixT = sb("mixT", [128, KC, 128], BF16)
        PS = [kb.T(es, f"PS{i}", [128, 512], F32, psum=True) for i in range(8)]

        class _V:
            def __init__(self, ap):
                self.ap = ap

            def __getitem__(self, k):
                return self.ap if k == slice(None) else self.ap[k]
        tt = _V(xacc[:].rearrange("p c t -> p (c t)"))
        ya = _V(xacc[:, 0:4, :].rearrange("p c t -> p (c t)"))
        yb = _V(xacc[:, 4:8, :].rearrange("p c t -> p (c t)"))
        fo = _V(xacc[:, 4:6, :].rearrange("p c (a d) -> p (c a) d", a=2))
        fo2 = _V(xacc[:, 6:8, :].rearrange("p c (a d) -> p (c a) d", a=2))

        def bf(ps_ap):
            return ps_ap.bitcast(BF16)

        def bc(ap, shape):
            return ap.unsqueeze(len(ap.shape)).to_broadcast(shape)

        cnt = [0]

        def evac_eng():
            cnt[0] += 1
            return "act" if cnt[0] % 2 else "dve"

        def copy(eng, out, in_):
            if eng == "act":
                kb.i("act", "activation", out=out, in_=in_, func=AF.Copy)
            else:
                kb.i(eng, "tensor_copy", out=out, in_=in_)

        def rstd(ss_ap, tmp_ap, out_ap, n, eps):
            kb.i("act", "activation", out=tmp_ap, in_=ss_ap, func=AF.Ln, scale=1.0 / n, bias=eps)
            kb.i("act", "activation", out=out_ap, in_=tmp_ap, func=AF.Exp, scale=-0.5)

        if upto < 7:
            kb.i("pool", "memset", ap=mix_tm[:], constant=0.0, _w=[kb.reg[mix_tm.name]])
        QF = []
        a_pre = [False]
        lr_pre = [False]

        def qf_hook():
            if QF:
                kb.replay(QF.pop(0))
        for seq in range(nseq):
            kb.i("pool", "memset", ap=ST[:], constant=0.0, _w=[kb.reg[ST.name]])
            kb.i("pool", "memset", ap=hst[:], constant=0.0, _w=[kb.reg[hst.name]])
            kb.i("pool", "memset", ap=hbf[:], constant=0.0, _w=[kb.reg[hbf.name]])
            kb.i("pool", "memset", ap=prefix[:], constant=0.0, _w=[kb.reg[prefix.name]])
            kb.i("pool", "memset", ap=uT[:], constant=0.0, _w=[kb.reg[uT.name]])
            kb.i("pool", "memset", ap=uTp[:], constant=0.0, _w=[kb.reg[uTp.name]])
            kb.i("pool", "memset", ap=xraw[:], constant=0.0, _w=[kb.reg[xraw.name]])
            kb.i("pool", "memset", ap=xcar[:], constant=0.0, _w=[kb.reg[xcar.name]])
            kb.d("sp", hb[0][:], h_in[seq * T:seq * T + 128, :], _r=[hin_bufs[seq * NB]])
            for blk in range(NB):
                s = blk % 2
                row0 = seq * T + blk * 128
                hbs = hb[s]
                kb.hook = qf_hook
                uTs = uT
                def stage_A(hbx, has_prev):
                    kb.i("act", "activation", out=u[:], in_=hbx[:], func=AF.Square, accum_out=st[:, 0:1])
                    rstd(st[:, 0:1], st[:, 1:2], st[:, 2:3], D, NORM_EPS)
                    kb.i("dve", "tensor_scalar", out=u[:], in0=hbx[:], scalar1=st[:, 2:3], scalar2=None, op0=ALU.mult)
                    psT_ = bf(PS[0][:]).rearrange("p (c t) -> p c t", c=8)
                    kb.pe([("transpose", dict(out=psT_[:, kc, :], in_=u[:, kc * 128:(kc + 1) * 128], identity=identb[:]))
                           for kc in range(KC)], _r=[b_c])
                    if has_prev:
                        kb.i("act", "activation", out=uTp[:, :, 0:1], in_=uT[:, :, 127:128], func=AF.Copy)
                    kb.i("dve", "tensor_copy", out=uT[:], in_=psT_)
                    kb.i("dve", "tensor_copy", out=uTp[:, :, 1:128], in_=psT_[:, :, 0:127])
                if not a_pre[0]:
                    stage_A(hbs, blk > 0)
                a_pre[0] = False
                cur = lambda kc: uT[:, kc, :]
                prv = lambda kc: uTp[:, kc, :]

                def proj_tm(ps_ap, c0, c1, shift):
                    calls = []
                    nmm = KC * (2 if shift else 1)
                    k_ = 0
                    for kc in range(KC):
                        calls.append(("matmul", dict(out=ps_ap, lhsT=cur(kc), rhs=Win[:, kc, c0:c1],
                                                     start=(k_ == 0), stop=(k_ == nmm - 1))))
                        k_ += 1
                        if shift:
                            calls.append(("matmul", dict(out=ps_ap, lhsT=prv(kc), rhs=WinB[:, kc, c0:c1],
                                                         start=False, stop=(k_ == nmm - 1))))
                            k_ += 1
                    if kb.rec is not None:
                        for a in range(0, len(calls), 2):
                            kb.pe(calls[a:a + 2])
                    else:
                        kb.pe(calls)

                def proj_fm(ps_ap, c0, shift=False):
                    calls = []
                    nmm = KC * (2 if shift else 1)
                    k_ = 0
                    for kc in range(KC):
                        calls.append(("matmul", dict(out=ps_ap, lhsT=Win[:, kc, c0:c0 + 128], rhs=cur(kc),
                                                     start=(k_ == 0), stop=(k_ == nmm - 1))))
                        k_ += 1
                        if shift:
                            calls.append(("matmul", dict(out=ps_ap, lhsT=WinB[:, kc, c0:c0 + 128],
                                                         rhs=prv(kc), start=False, stop=(k_ == nmm - 1))))
                            k_ += 1
                    if kb.rec is not None:
                        for a in range(0, len(calls), 4):
                            kb.pe(calls[a:a + 4])
                    else:
                        kb.pe(calls)

                def B_rwkv():
                    proj_tm(PS[1][:, 0:512], 0, 512, True)
                    copy("act", rk[:], PS[1][:, 0:512])
                    proj_tm(PS[2][:, 0:256], 512, 768, True)
                    copy("dve", v_tm[:], PS[2][:, 0:256])
                    if not lr_pre[0]:
                        B_lr(PS[3])
                    lr_pre[0] = False

                def B_lr(bk_):
                    proj_fm(bk_[:, 0:128], 768, True)
                    kb.i("act", "activation", out=lr_fm[0:32, :], in_=bk_[0:32, 0:128], func=AF.Tanh)
                    kb.i("act", "activation", out=lr_fm[32:64, :], in_=bk_[32:64, 0:128], func=AF.Copy)
                    kb.i("act", "activation", out=lr_fm[64:128, :], in_=bk_[64:128, 0:128], func=AF.Sigmoid)

                def B_rest(bi, bz, bt, bq):
                    proj_tm(bz[:, 0:512], 896, 1408, False)
                    kb.i("act", "activation", out=zs[:], in_=bz[:, 0:512], func=AF.Silu)
                    proj_tm(bt[:, 0:268], 2944, 3212, False)
                    kb.i("dve", "tensor_copy", out=VA[bi][:, :, 0:64], in_=bt[:, 0:256].rearrange("p (h d) -> p h d", h=4))
                    kb.i("pool", "memset", ap=VA[bi][:, :, 64:65], constant=1.0, _w=[kb.reg[VA[bi].name]])
                    kb.i("dve", "tensor_copy", out=tok8[:, 0:12], in_=bt[:, 256:268])
                    kb.i("dve", "tensor_copy", out=xraw[:, :, 0:3], in_=xcar[:, :, 0:3])
                    for c4 in range(2):
                        pst = (bz, bt)[c4]
                        for c in range(4):
                            proj_fm(pst[:, c * 128:(c + 1) * 128], 1408 + (c4 * 4 + c) * 128)
                        copy("act" if c4 else "dve", xraw[:, c4 * 4:(c4 + 1) * 4, 3:131],
                             pst[:].rearrange("p (c t) -> p c t", c=4))
                    for c in range(4):
                        proj_fm(bq[:, c * 128:(c + 1) * 128], 2432 + c * 128)
                    copy("act", qT[:], bq[:, 0:256].rearrange("p (c t) -> p c t", c=2))
                    copy("dve", KT[bi][:], bq[:, 256:512].rearrange("p (c t) -> p c t", c=2))
                xr = xraw
                B_rwkv()
                kb.hook = None
                while QF:
                    kb.replay(QF.pop(0))
                if blk + 1 < NB:
                    kb.d("sp", hb[1 - s][:], h_in[row0 + 128:row0 + 256, :], _r=[hin_bufs[seq * NB + blk + 1]])
                kb.rec = []
                B_rest(blk, PS[6], PS[7], PS[5])
                Q0 = kb.rec
                kb.rec = []
                if upto >= 6:
                    TT0 = lambda e_, o_, a_, b_, op_: kb.i(e_, "tensor_tensor", out=o_, in0=a_, in1=b_, op=op_)
                    mmf0 = lambda **kw: ("matmul", dict(start=True, stop=True, **kw))
                    TT0("dve", tok8[:, 12:16], tok8[:, 0:4], fb4[:], ALU.add)
                    kb.i("act", "activation", out=tok8[:, 12:16], in_=tok8[:, 12:16], func=AF.Exp, scale=-1.0)
                    kb.i("act", "activation", out=tok8[:, 12:16], in_=tok8[:, 12:16], func=AF.Ln, bias=1.0)
                    kb.pe([mmf0(out=PS[5][:, 16:20], lhsT=triI[:], rhs=tok8[:, 12:16]),
                           mmf0(out=PS[5][:, 20:24], lhsT=ones[:], rhs=tok8[:, 12:16]),
                           mmf0(out=PS[5][:, 24:28], lhsT=halfm[:], rhs=tok8[:, 12:16])])
                    TT0("dve", ckg[:, blk, :], PS[5][:, 16:20], prefix[:], ALU.add)
                    TT0("dve", refn[:], PS[5][:, 24:28], prefix[:], ALU.add)
                    TT0("dve", prefix[:], PS[5][:, 20:24], prefix[:], ALU.add)
                    TT0("dve", ball[:, 0:blk + 1, :], ckg[:, 0:blk + 1, :],
                        refn[:].unsqueeze(1).to_broadcast([128, blk + 1, 4]), ALU.subtract)
                fox_groups = []
                if upto >= 6:
                    for h in range(4):
                        for j0 in range(0, blk + 1, 4):
                            fox_groups.append((h, list(range(j0, min(j0 + 4, blk + 1)))))
                psS2 = [PS[0], PS[0]]

                def fox_S(k):
                    h, js = fox_groups[k]
                    hp, q_ = h // 2, h % 2
                    pr = slice(64 * q_, 64 * q_ + 64)
                    psS = psS2[k % 2]
                    calls = []
                    for jj, j in enumerate(js):
                        calls.append(("matmul", dict(out=psS[:, jj * 128:(jj + 1) * 128], lhsT=KT[j][pr, hp, :],
                                                     rhs=qT[pr, hp, :], start=True, stop=(j != blk))))
                        if j == blk:
                            calls.append(("matmul", dict(out=psS[:, jj * 128:(jj + 1) * 128], lhsT=identb[:],
                                                         rhs=NEGb[:], start=False, stop=True)))
                    kb.pe(calls, _r=[b_c])

                def fox_E(k):
                    h, js = fox_groups[k]
                    psS = psS2[k % 2]
                    for jj, j in enumerate(js):
                        kb.i("act", "activation", out=Et[k % 2][:, jj, :], in_=psS[:, jj * 128:(jj + 1) * 128],
                             func=AF.Exp, scale=0.125, bias=ball[:, j, h:h + 1])

                def fox_PV(k):
                    h, js = fox_groups[k]
                    kb.pe([("matmul", dict(out=PS[5][:, h * 128:h * 128 + 65], lhsT=Et[k % 2][:, jj, :],
                                           rhs=VA[j][:, h, 0:65], start=(j == 0), stop=(j == blk)))
                           for jj, j in enumerate(js)])
                fox_state = [0]
                n_fg = len(fox_groups)
                if n_fg:
                    fox_S(0)
                n_fox_pts = 15
                fox_per = -(-(n_fg + 2) // n_fox_pts) if n_fg else 0

                def fox_pull(n=None):
                    n = fox_per if n is None else n
                    for _ in range(n):
                        k = fox_state[0]
                        if k >= n_fg + 1 or n_fg == 0:
                            return
                        if k < n_fg:
                            fox_E(k)
                        if k + 1 < n_fg:
                            fox_S(k + 1)
                        if k >= 1:
                            fox_PV(k - 1)
                        fox_state[0] = k + 1

                def ssd_gen():
                    if upto >= 3:
                        for c in range(8):
                            kb.i("dve", "tensor_scalar", out=xacc[:, c, :], in0=xr[:, c, 0:128], scalar1=cw[:, c, 0:1],
                                 scalar2=cb[:, c:c + 1], op0=ALU.mult, op1=ALU.add)
                            for k_ in range(1, 4):
                                kb.i("dve", "scalar_tensor_tensor", out=xacc[:, c, :], in0=xr[:, c, k_:k_ + 128],
                                     scalar=cw[:, c, k_:k_ + 1], in1=xacc[:, c, :], op0=ALU.mult, op1=ALU.add)
                        kb.i("dve", "tensor_copy", out=xcar[:, :, 0:3], in_=xr[:, :, 128:131])
                        kb.i("act", "activation", out=xbcT[:], in_=xacc[:], func=AF.Silu)
                        psT = bf(PS[6][:]).rearrange("p (c t) -> p c t", c=8)
                        kb.pe([("transpose", dict(out=psT[:, c, :], in_=xbcT[:, c, :], identity=identb[:])) for c in range(6)],
                              _r=[b_c])
                        kb.i("dve", "tensor_copy", out=x_tm[:], in_=bf(PS[6][:])[:, 0:512])
                        kb.i("act", "activation", out=B_tm[:], in_=bf(PS[6][:])[:, 512:768], func=AF.Copy)
                    if upto < 5:
                        return
                    TT_ = lambda e_, o_, a_, b_, op_: kb.i(e_, "tensor_tensor", out=o_, in0=a_, in1=b_, op=op_)
                    mmf = lambda **kw: ("matmul", dict(start=True, stop=True, **kw))
                    h8 = lambda ap: ap.rearrange("p (h d) -> p h d", h=8)
                    TT_("dve", dts[:, 0, :], tok8[:, 4:12], sp8[:, 0, :], ALU.add)
                    kb.i("act", "activation", out=dts[:, 7, :], in_=dts[:, 0, :], func=AF.Exp)
                    kb.i("act", "activation", out=dts[:, 0, :], in_=dts[:, 7, :], func=AF.Ln, bias=1.0)
                    TT_("dve", dts[:, 1, :], dts[:, 0, :], sp8[:, 1, :], ALU.mult)
                    yield
                    kb.pe([mmf(out=PS[7][:, 0:8], lhsT=triI[:], rhs=dts[:, 1, :]),
                           mmf(out=PS[7][:, 8:16], lhsT=ones[:], rhs=dts[:, 1, :])])
                    copy("dve", dts[:, 2, :], PS[7][:, 0:8])
                    kb.i("dve", "tensor_scalar", out=dts[:, 6, :], in0=PS[7][:, 0:8], scalar1=-1.0, scalar2=None, op0=ALU.mult)
                    kb.i("act", "activation", out=dts[:, 3, :], in_=PS[7][:, 0:8], func=AF.Exp)
                    TT_("dve", dts[:, 4, :], PS[7][:, 8:16], dts[:, 2, :], ALU.subtract)
                    kb.i("act", "activation", out=dts[:, 4, :], in_=dts[:, 4, :], func=AF.Exp)
                    TT_("dve", dts[:, 5, :], dts[:, 0, :], dts[:, 4, :], ALU.mult)
                    kb.i("act", "activation", out=dts[:, 7, :], in_=PS[7][:, 8:16], func=AF.Exp)
                    yield
                    kb.i("pool", "tensor_copy", out=dAb[:], in_=bc(dts[:, 1, :], [128, 8, 128]))
                    kb.pe([mmf(out=PS[6][:, g_ * 128:(g_ + 1) * 128], lhsT=xbcT[:, 4 + g_, :], rhs=xbcT[:, 6 + g_, :])
                           for g_ in range(2)])
                    TT_("dve", h8(xdt[:]), h8(x_tm[:]), bc(dts[:, 0, :], [128, 8, 64]), ALU.mult)
                    TT_("pool", h8(xdte[:]), h8(x_tm[:]), bc(dts[:, 5, :], [128, 8, 64]), ALU.mult)
                    yield
                    for h in range(8):
                        g_ = h // 4
                        bk = PS[6][:, 256 + (h % 2) * 128:384 + (h % 2) * 128]
                        kb.pe([("matmul", dict(out=bk, lhsT=dAb[:, h, :], rhs=triI[:], start=True, stop=False)),
                               ("matmul", dict(out=bk, lhsT=identf[:], rhs=NEGf[:], start=False, stop=True))],
                              _r=[b_c])
                        kb.i("act", "activation", out=dec[h % 2][:], in_=bk, func=AF.Exp, bias=dts[:, 6, h:h + 1])
                        TT_("dve", Lh[:, h, :], PS[6][:, g_ * 128:(g_ + 1) * 128], dec[h % 2][:], ALU.mult)
                        yield
                    kb.pe([mmf(out=PS[7][:, h * 64:(h + 1) * 64], lhsT=Lh[:, h, :], rhs=xdt[:, h * 64:(h + 1) * 64])
                           for h in range(8)])
                    kb.pe([mmf(out=PS[6][:, h * 64:(h + 1) * 64], lhsT=xbcT[:, 6 + h // 4, :], rhs=hbf[:, h * 64:(h + 1) * 64])
                           for h in range(8)])
                    TT_("dve", h8(ya[:]), h8(PS[6][:]), bc(dts[:, 3, :], [128, 8, 64]), ALU.mult)
                    TT_("dve", ya[:], PS[7][:], ya[:], ALU.add)
                    yield
                    TT_("pool", h8(yb[:]), h8(x_tm[:]), bc(sp8[:, 2, :], [128, 8, 64]), ALU.mult)
                    TT_("pool", ya[:], ya[:], yb[:], ALU.add)
                    TT_("dve", ya[:], ya[:], zs[:], ALU.mult)
                    yield
                    TT_("pool", yb[:], ya[:], ya[:], ALU.mult)
                    kb.i("dve", "tensor_reduce", out=st[:, 8:10], in_=yb[:].rearrange("p (g d) -> p g d", g=2), axis=AX.X,
                         op=ALU.add)
                    rstd(st[:, 8:10], st[:, 10:12], st[:, 12:14], 256, SSM_NORM_EPS)
                    yield
                    TT_("dve", mix_tm[:, 256:768].rearrange("p (g d) -> p g d", g=2),
                        ya[:].rearrange("p (g d) -> p g d", g=2), bc(st[:, 12:14], [128, 2, 256]), ALU.mult)
                    kb.pe([mmf(out=PS[7][:, h * 64:(h + 1) * 64], lhsT=B_tm[:, (h // 4) * 128:(h // 4 + 1) * 128],
                               rhs=xdte[:, h * 64:(h + 1) * 64]) for h in range(8)])
                    TT_("dve", h8(hst[:]), h8(hst[:]), bc(dts[:, 7, :], [128, 8, 64]), ALU.mult)
                    TT_("dve", hst[:], PS[7][:], hst[:], ALU.add)
                    copy("act", hbf[:], hst[:])
                def ssd_pull(n=1):
                    pass

                fox_pull(10 ** 6)
                if upto >= 6:
                    fo_ = Et[0][:].bitcast(F32)
                    fo2_ = Et[1][:].bitcast(F32)
                    ov = PS[5][:].rearrange("p (h c) -> p h c", h=4)
                    kb.i("dve", "tensor_copy", out=st[:, 16:20], in_=ov[:, :, 64])
                    kb.i("dve", "reciprocal", out=st[:, 20:24], in_=st[:, 16:20])
                    TT0("dve", fo_, ov[:, :, 0:64], bc(st[:, 20:24], [128, 4, 64]), ALU.mult)
                    TT0("pool", fo2_, fo_, fo_, ALU.mult)
                    kb.i("dve", "tensor_reduce", out=st[:, 24:28], in_=fo2_, axis=AX.X, op=ALU.add)
                    rstd(st[:, 24:28], st[:, 28:32], st[:, 24:28], 64, NORM_EPS)
                    TT0("dve", mix_tm[:, 768:1024].rearrange("p (h d) -> p h d", h=4), fo_,
                        bc(st[:, 24:28], [128, 4, 64]), ALU.mult)
                if blk + 1 < NB and upto >= 7:
                    stage_A(hb[1 - s], True)
                    a_pre[0] = True
                    B_lr(PS[0])
                    lr_pre[0] = True
                Q2 = kb.rec
                kb.rec = []
                for _ in ssd_gen():
                    pass
                Q1 = kb.rec
                kb.rec = None
                N_RWKV = 210
                rw_done = [0]

                def il_hook():
                    rw_done[0] += 1
                    if Q0:
                        for _ in range(-(-len(Q0) // max(1, 30 - rw_done[0]))):
                            if Q0:
                                kb.replay(Q0.pop(0))
                        return
                    left = max(1, N_RWKV - rw_done[0])
                    for Q in (Q1, Q2):
                        for _ in range(-(-len(Q) // left)):
                            if Q:
                                kb.replay(Q.pop(0))
                kb.hook = il_hook
                if upto >= 4:
                    r_ = rk[:, 0:256]
                    k_r = rk[:, 256:512]
                    pb = lambda i_: pbc[:, i_, :]
                    h4 = lambda ap: ap.rearrange("p (h d) -> p h d", h=4)
                    TT_ = lambda e_, o_, a_, b_, op_: kb.i(e_, "tensor_tensor", out=o_, in0=a_, in1=b_, op=op_)
                    mmf = lambda **kw: ("matmul", dict(start=True, stop=True, **kw))
                    kb.pe([mmf(out=PS[1][:, 0:256], lhsT=lr_fm[0:32, :], rhs=lr2[0:32, :]),
                           mmf(out=PS[2][:, 0:256], lhsT=lr_fm[32:64, :], rhs=lr2[32:64, :]),
                           mmf(out=PS[3][:, 0:256], lhsT=lr_fm[64:128, :], rhs=lr2[64:128, :])])
                    TT_("dve", R["t1"][:], PS[1][:, 0:256], pb(0), ALU.add)
                    kb.i("act", "activation", out=R["sg"][:], in_=R["t1"][:], func=AF.Sigmoid)
                    TT_("dve", R["t2"][:], PS[2][:, 0:256], pb(1), ALU.add)
                    kb.i("act", "activation", out=R["a"][:], in_=R["t2"][:], func=AF.Sigmoid)
                    copy("act", R["g"][:], PS[3][:, 0:256])
                    TT_("pool", R["kk"][:], k_r, pb(2), ALU.mult)
                    TT_("dve", R["t1"][:], R["kk"][:], R["kk"][:], ALU.mult)
                    kb.i("dve", "tensor_reduce", out=st4[:, 0:4], in_=h4(R["t1"][:]), axis=AX.X, op=ALU.add)
                    kb.i("act", "activation", out=st4[:, 4:8], in_=st4[:, 0:4], func=AF.Ln, bias=1e-30)
                    kb.i("act", "activation", out=st4[:, 8:12], in_=st4[:, 4:8], func=AF.Exp, scale=-0.5)
                    TT_("dve", h4(R["kkn"][:]), h4(R["kk"][:]), bc(st4[:, 8:12], [128, 4, 64]), ALU.mult)
                    kb.i("dve", "scalar_tensor_tensor", out=R["t2"][:], in0=R["a"][:], scalar=-1.0, in1=pb(3),
                         op0=ALU.add, op1=ALU.mult)
                    kb.i("dve", "scalar_tensor_tensor", out=R["km"][:], in0=R["t2"][:], scalar=1.0, in1=k_r,
                         op0=ALU.add, op1=ALU.mult)
                    TT_("pool", R["b"][:], R["kkn"][:], R["a"][:], ALU.mult)
                    kb.pe([mmf(out=PS[1][:, 0:256], lhsT=triI[:], rhs=R["sg"][:]),
                           mmf(out=PS[1][:, 256:512], lhsT=ones[:], rhs=R["sg"][:]),
                           mmf(out=PS[2][:, 256:257], lhsT=R["sg"][:, 0:128], rhs=ones[:, 0:1]),
                           mmf(out=PS[2][:, 257:258], lhsT=R["sg"][:, 128:256], rhs=ones[:, 0:1])])
                    copy("dve", R["cs"][:], PS[1][:, 0:256])
                    kb.i("act", "activation", out=GC[:], in_=PS[2][:, 256:258], func=AF.Exp, scale=-C0)
                    kb.i("act", "activation", out=R["E"][:], in_=R["cs"][:], func=AF.Exp, scale=-C0)
                    kb.i("act", "activation", out=R["E2"][:], in_=R["cs"][:], func=AF.Exp, scale=C0)
                    TT_("dve", R["t1"][:], R["cs"][:], R["sg"][:], ALU.subtract)
                    TT_("dve", R["t2"][:], PS[1][:, 256:512], R["cs"][:], ALU.subtract)
                    kb.i("act", "activation", out=R["E3"][:], in_=R["t1"][:], func=AF.Exp, scale=-C0)
                    TT_("dve", R["rt"][:], r_, R["E"][:], ALU.mult)
                    TT_("dve", R["kt"][:], R["km"][:], R["E2"][:], ALU.mult)
                    TT_("pool", R["bt"][:], R["b"][:], R["E2"][:], ALU.mult)
                    kb.i("act", "activation", out=R["E"][:], in_=R["t2"][:], func=AF.Exp, scale=-C0)
                    kb.i("dve", "scalar_tensor_tensor", out=R["at"][:], in0=R["kkn"][:], scalar=-1.0, in1=R["E3"][:],
                         op0=ALU.mult, op1=ALU.mult)
                    TT_("dve", R["Kd"][:], R["km"][:], R["E"][:], ALU.mult)
                    TT_("pool", R["Bd"][:], R["b"][:], R["E"][:], ALU.mult)
                    TT_("dve", R["t1"][:], r_, R["km"][:], ALU.mult)
                    TT_("dve", R["t2"][:], R["t1"][:], pb(4), ALU.mult)
                    kb.i("dve", "tensor_reduce", out=st4[:, 12:16], in_=h4(R["t2"][:]), axis=AX.X, op=ALU.add)
                    for bank, names, FM in ((PS[3], ("at", "rt"), FMa), (PS[4], ("bt", "kt"), FMb)):
                        kb.pe([("transpose", dict(out=bank[:, (hp * 2 + x) * 128:(hp * 2 + x + 1) * 128],
                                                  in_=R[names[x]][:, hp * 128:(hp + 1) * 128], identity=identf[:]))
                               for hp in range(2) for x in range(2)], _r=[b_c])
                        copy(evac_eng(), FM[:].rearrange("p a b t -> p (a b t)"), bank[:])
                    for h in range(4):
                        hp, q_ = h // 2, h % 2
                        pr = slice(64 * q_, 64 * q_ + 64)
                        bA, bB = (PS[1], PS[2]) if h % 2 == 0 else (PS[3], PS[4])
                        ar = FMa[pr, hp, :, :].rearrange("p x t -> p (x t)")
                        bk_ = FMb[pr, hp, :, :].rearrange("p x t -> p (x t)")
                        kb.pe([mmf(out=bA[:, 0:256], lhsT=FMb[pr, hp, 0, :], rhs=ar),
                               mmf(out=bA[:, 256:384], lhsT=FMb[pr, hp, 1, :], rhs=FMa[pr, hp, 1, :]),
                               mmf(out=bB[:, 0:256], lhsT=FMa[pr, hp, 0, :], rhs=bk_)])
                        TT_("dve", Wv[h][:, 128:256], bA[:, 0:128], maskA[:, 0:128], ALU.mult)
                        TT_("dve", SA[h][:, 0:256], bA[:, 128:384], maskA[:, 128:384], ALU.mult)
                        TT_("dve", Wv[h][:, 0:128], bB[:, 0:128], maskB[:, 0:128], ALU.mult)
                        TT_("dve", SBt[h][:, 0:128], bB[:, 128:256], maskB[:, 128:256], ALU.mult)
                        kb.i("pool", "tensor_copy", out=Wv[h][:, 256:384], in_=identf[:], _r=[b_c])
                    for lv in range(1, 8):
                        last = (lv == 7)
                        for h in range(4):
                            bk = PS[1 + h]
                            if not last:
                                kb.pe([mmf(out=bk[:, 128:384], lhsT=Wv[h][:, 0:128], rhs=Wv[h][:, 128:384]),
                                       mmf(out=bk[:, 0:128], lhsT=Wv[h][:, 128:256], rhs=Wv[h][:, 0:128])])
                            else:
                                kb.pe([mmf(out=bk[:, 256:384], lhsT=Wv[h][:, 0:128], rhs=Wv[h][:, 256:384])])
                        for h in range(4):
                            bk = PS[1 + h]
                            if not last:
                                copy("act", Wv[h][:, 0:256], bk[:, 0:256])
                            TT_("dve", Wv[h][:, 256:384], bk[:, 256:384], Wv[h][:, 256:384], ALU.add)
                        fox_pull()
                        ssd_pull()
                    for h in range(4):
                        hp, q_ = h // 2, h % 2
                        pr = slice(64 * q_, 64 * q_ + 64)
                        bk = PS[1 + h]
                        kb.pe([mmf(out=bk[:, 0:128], lhsT=SBt[h][:, 0:128], rhs=Wv[h][:, 256:384]),
                               mmf(out=bk[pr, 128:256], lhsT=R["at"][:, h * 64:(h + 1) * 64], rhs=Wv[h][:, 256:384])])
                        copy("act", Mk2[h][:], bk[:, 0:128])
                        copy("dve", Wt[pr, hp, :], bk[pr, 128:256])
                        fox_pull()
                        ssd_pull()
                    def hd(h):
                        hp, q_ = h // 2, h % 2
                        return (hp, q_, slice(64 * q_, 64 * q_ + 64), slice(h * 64, (h + 1) * 64),
                                PS[1] if q_ == 0 else PS[2], PS[3] if q_ == 0 else PS[4],
                                slice(hp * 64, hp * 64 + 64), slice(256 + hp * 64, 256 + hp * 64 + 64))
                    for h in range(4):
                        hp, q_, pr, hs, bU, bY, us, so = hd(h)
                        kb.pe([("matmul", dict(out=bU[:, us], lhsT=Wt[pr, hp, :], rhs=ST[pr, hp, :], start=True, stop=False)),
                               ("matmul", dict(out=bU[:, us], lhsT=Mk2[h][:], rhs=v_tm[:, hs], start=False, stop=True))])
                    for q_ in range(2):
                        copy("act" if q_ == 0 else "dve",
                             U_tm[:].rearrange("p (hp q d) -> p hp q d", hp=2, q=2)[:, :, q_, :],
                             PS[1 + q_][:, 0:128].rearrange("p (hp d) -> p hp d", hp=2))
                    fox_pull()
                    ssd_pull()
                    for h in range(4):
                        hp, q_, pr, hs, bU, bY, us, so = hd(h)
                        kb.pe([("matmul", dict(out=bY[:, us], lhsT=FMa[pr, hp, 1, :], rhs=ST[pr, hp, :], start=True, stop=False)),
                               ("matmul", dict(out=bY[:, us], lhsT=SA[h][:, 128:256], rhs=v_tm[:, hs], start=False, stop=False)),
                               ("matmul", dict(out=bY[:, us], lhsT=SA[h][:, 0:128], rhs=U_tm[:, hs], start=False, stop=True))])
                        kb.pe([("matmul", dict(out=bU[pr, so], lhsT=R["Kd"][:, hs], rhs=v_tm[:, hs], start=True, stop=False)),
                               ("matmul", dict(out=bU[pr, so], lhsT=R["Bd"][:, hs], rhs=U_tm[:, hs], start=False, stop=True))])
                    for h in range(4):
                        hp, q_, pr, hs, bU, bY, us, so = hd(h)
                        kb.i("dve", "scalar_tensor_tensor", out=ST[pr, hp, :], in0=ST[pr, hp, :], scalar=GC[pr, hp:hp + 1],
                             in1=bU[pr, so], op0=ALU.mult, op1=ALU.add)
                    fox_pull()
                    ssd_pull()
                    fox_pull()
                    ssd_pull()
                    fox_pull()
                    ssd_pull()
                    for q_ in range(2):
                        copy("act", R["ysb"][:].rearrange("p (hp q d) -> p hp q d", hp=2, q=2)[:, :, q_, :],
                             PS[3 + q_][:, 0:128].rearrange("p (hp d) -> p hp d", hp=2))
                    kb.i("dve", "tensor_reduce", out=st4[:, 16:20], in_=h4(R["ysb"][:]), axis=AX.X, op=ALU.add)
                    kb.i("dve", "tensor_scalar", out=st4[:, 16:20], in0=st4[:, 16:20], scalar1=1.0 / 64, scalar2=None,
                         op0=ALU.mult)
                    TT_("dve", h4(R["yc"][:]), h4(R["ysb"][:]), bc(st4[:, 16:20], [128, 4, 64]), ALU.subtract)
                    TT_("pool", R["t1"][:], R["yc"][:], R["yc"][:], ALU.mult)
                    kb.i("dve", "tensor_reduce", out=st4[:, 20:24], in_=h4(R["t1"][:]), axis=AX.X, op=ALU.add)
                    rstd(st4[:, 20:24], st4[:, 24:28], st4[:, 28:32], 64, RWKV_GN_EPS)
                    TT_("dve", h4(R["yc"][:]), h4(R["yc"][:]), bc(st4[:, 28:32], [128, 4, 64]), ALU.mult)
                    TT_("pool", R["yc"][:], R["yc"][:], pb(5), ALU.mult)
                    TT_("pool", R["yc"][:], R["yc"][:], pb(6), ALU.add)
                    TT_("dve", h4(R["t2"][:]), h4(v_tm[:]), bc(st4[:, 12:16], [128, 4, 64]), ALU.mult)
                    TT_("dve", R["yc"][:], R["yc"][:], R["t2"][:], ALU.add)
                    TT_("dve", mix_tm[:, 0:256], R["yc"][:], R["g"][:], ALU.mult)
                kb.hook = None
                while Q0:
                    kb.replay(Q0.pop(0))
                while Q1:
                    kb.replay(Q1.pop(0))
                while Q2:
                    kb.replay(Q2.pop(0))
                kb.rec = []
                psT = bf(PS[6][:]).rearrange("p (c t) -> p c t", c=8)
                kb.pe([("transpose", dict(out=psT[:, c, :], in_=mix_tm[:, c * 128:(c + 1) * 128], identity=identb[:]))
                       for c in range(KC)], _r=[b_c])
                kb.i("dve", "tensor_copy", out=mixT[:], in_=psT)
                for dn in range(2):
                    kb.pe([("matmul", dict(out=PS[6 + dn][:], lhsT=mixT[:, c, :],
                                           rhs=Wout[:, c, dn * 512:(dn + 1) * 512], start=(c == 0), stop=(c == KC - 1)))
                           for c in range(KC)])
                for dn in range(2):
                    kb.i("act", "activation", out=tt[:][:, dn * 512:(dn + 1) * 512], in_=PS[6 + dn][:], func=AF.Square,
                         accum_out=st[:, 4 + dn:5 + dn])
                kb.i("dve", "tensor_tensor", out=st[:, 4:5], in0=st[:, 4:5], in1=st[:, 5:6], op=ALU.add)
                rstd(st[:, 4:5], st[:, 5:6], st[:, 6:7], D, NORM_EPS)
                for dn in range(2):
                    kb.i("dve", "scalar_tensor_tensor", out=tt[:][:, dn * 512:(dn + 1) * 512], in0=PS[6 + dn][:],
                         scalar=st[:, 6:7], in1=gpost[:, dn * 512:(dn + 1) * 512], op0=ALU.mult, op1=ALU.mult)
                kb.i("pool", "tensor_tensor", out=hbs[:], in0=tt[:], in1=hbs[:], op=ALU.add)
                if dbg is not None:
                    kb.d("sp", dbg["mix"][row0:row0 + 128, :], mix_tm[:])
                kb.d("sp", h_out[row0:row0 + 128, :], hbs[:], _w=[hout_bufs[seq * NB + blk]])
                QF.extend(kb.rec)
                kb.rec = None
        while QF:
            kb.replay(QF.pop(0))
        kb.barrier()


PNAMES = ["norm_mix_pre", "norm_mix_post", "w_in", "rwkv_mu", "rwkv_w0", "rwkv_w2", "rwkv_a0", "rwkv_a2", "rwkv_g2",
          "rwkv_k_k", "rwkv_k_a", "rwkv_r_k", "rwkv_ln_w", "rwkv_ln_b", "ssm_conv_w", "ssm_conv_b", "ssm_dt_bias",
          "ssm_A_log", "ssm_D", "ssm_norm_w", "fox_f_bias", "fox_norm_w", "w_out", "norm_mlp_pre", "norm_mlp_post",
          "w_mlp_up", "w_mlp_down"]
N_CORES = 8
_CACHE = {}


def build_program(shapes, nseq, T, depth, phases=None):
    nc = bass.Bass("TRN2", target_bir_lowering=False, dynamic_dma_scratch_size=1024)
    kb = KB(nc)
    ntok = nseq * T
    x = nc.dram_tensor("x", [ntok, D], F32, kind="ExternalInput").ap()
    out = nc.dram_tensor("out", [ntok, D], F32, kind="ExternalOutput").ap()
    P = {n: nc.dram_tensor(n, list(shapes[n]), F32, kind="ExternalInput").ap() for n in PNAMES}
    hA = nc.dram_tensor("hA", [ntok, D], F32, kind="Internal").ap()
    hB = nc.dram_tensor("hB", [ntok, D], F32, kind="Internal").ap()
    nchunk = ntok // 128
    cst = make_consts(kb, nc)
    seqp = []
    for l in range(depth):
        seqp += [("mix", l), ("mlp", l)]
    if phases is not None:
        seqp = seqp[:phases]
    cur, cur_b = x, [Buf(f"x{i}") for i in range(nchunk)]
    scratch = [hA, hB]
    for pi, (kind, l) in enumerate(seqp):
        last = (pi == len(seqp) - 1)
        dst = out if last else scratch[pi % 2]
        dst_b = [Buf(f"h{pi}_{i}") for i in range(nchunk)]
        if kind == "mix":
            mix_phase(kb, nc, cst, cur, dst, cur_b, dst_b, P, l, nseq, T)
        else:
            mlp_phase(kb, nc, cst, cur, dst, cur_b, dst_b, P["w_mlp_up"][l], P["w_mlp_down"][l],
                      P["norm_mlp_pre"][l], P["norm_mlp_post"][l], ntok)
            kb.barrier()
        cur, cur_b = dst, dst_b
    kb.wait_all("sp", cur_b)
    return nc, kb


def kernel(**inputs):
    from concourse.bass_utils import run_bass_kernel_spmd
    x = np.ascontiguousarray(np.asarray(inputs["x"], dtype=np.float32))
    Bsz, T, Dm = x.shape
    depth = inputs["w_in"].shape[0]
    nseq = Bsz // N_CORES
    params = {n: np.ascontiguousarray(np.asarray(inputs[n], dtype=np.float32)) for n in PNAMES}
    key = (Bsz, T, depth)
    if key not in _CACHE:
        _CACHE[key] = build_program({n: params[n].shape for n in PNAMES}, nseq, T, depth)
    nc, kb = _CACHE[key]
    in_maps = []
    for c in range(N_CORES):
        m = dict(params)
        m["x"] = x[c * nseq:(c + 1) * nseq].reshape(nseq * T, Dm)
        in_maps.append(m)
    res = run_bass_kernel_spmd(nc, in_maps, core_ids=list(range(N_CORES)))
    outs = [np.asarray(res.results[c]["out"]).reshape(nseq, T, Dm) for c in range(N_CORES)]
    return np.concatenate(outs, axis=0).astype(np.float32)
```
